# Optimizing a Trainium2 kernel written in Bass

```python
import jax, jax.numpy as jnp
from jax import lax
import numpy as np

D_MODEL = 1024
BATCH = 32
SEQ = 2048
DEPTH = 2
DEC_BATCH = 8
DEC_SEQ = 32
PAST_LEN = 4096

CHUNK = 64
Q_BLOCK = 128
D_PLE = 256
D_CONV = D_MODEL // 2
CONV_W = 3
V_DIM = 64
QK_NOPE = 64
QK_ROPE = 32
QK_DIM = QK_NOPE + QK_ROPE
N_HEADS = (D_MODEL // 2) // V_DIM
D_ATTN = N_HEADS * V_DIM
Q_LORA = 12 * V_DIM
KV_LORA = 4 * V_DIM
D_MIX = D_CONV + D_ATTN
D_FF = -(-8 * D_MODEL // (3 * 256)) * 256
ROPE_THETA = 10000.0
EPS = 1e-6
SCALE = QK_DIM ** -0.5

OFF_B = 0
OFF_C = OFF_B + D_CONV
OFF_X = OFF_C + D_CONV
OFF_Q = OFF_X + D_CONV
OFF_KV = OFF_Q + Q_LORA
OFF_KR = OFF_KV + KV_LORA
IN_COLS = OFF_KR + QK_ROPE

kernel_name = "hymba_conv_mla_streaming_step"


def rms_norm(x, g):
    xf = x.astype(jnp.float32)
    y = xf * lax.rsqrt(jnp.mean(xf * xf, axis=-1, keepdims=True) + EPS)
    return (y * g.astype(jnp.float32)).astype(x.dtype)


def apply_rope(x, pos):
    half = x.shape[-1] // 2
    inv = ROPE_THETA ** (-jnp.arange(half, dtype=jnp.float32) / half)
    ang = pos.astype(jnp.float32)[:, None] * inv[None, :]
    ang = ang.reshape(ang.shape[:1] + (1,) * (x.ndim - 3) + (half,))
    cos, sin = jnp.cos(ang), jnp.sin(ang)
    xf = x.astype(jnp.float32)
    x1, x2 = xf[..., :half], xf[..., half:]
    return jnp.concatenate([x1 * cos - x2 * sin, x1 * sin + x2 * cos], axis=-1).astype(x.dtype)


def attend(q_nope, q_rope, k_nope, k_rope, v, mask):
    s = (jnp.einsum('nqhd,nkhd->nhqk', q_nope, k_nope)
         + jnp.einsum('nqhr,nkr->nhqk', q_rope, k_rope)).astype(jnp.float32) * SCALE
    if mask is not None:
        s = jnp.where(mask, s, jnp.finfo(jnp.float32).min)
    pr = jax.nn.softmax(s, axis=-1).astype(v.dtype)
    return jnp.einsum('nhqk,nkhd->nqhd', pr, v)


def prompt_attention(q_nope, q_rope, k_nope, k_rope, v):
    n, s = q_nope.shape[0], q_nope.shape[1]
    nb = s // Q_BLOCK
    qn = q_nope.reshape(n, nb, Q_BLOCK, N_HEADS, QK_NOPE).transpose(1, 0, 2, 3, 4)
    qr = q_rope.reshape(n, nb, Q_BLOCK, N_HEADS, QK_ROPE).transpose(1, 0, 2, 3, 4)
    key_chunk = jnp.arange(s) // CHUNK

    def block(args):
        qn_b, qr_b, bi = args
        q_chunk = (bi * Q_BLOCK + jnp.arange(Q_BLOCK)) // CHUNK
        mask = key_chunk[None, :] <= q_chunk[:, None]
        return attend(qn_b, qr_b, k_nope, k_rope, v, mask)

    out = lax.map(block, (qn, qr, jnp.arange(nb)))
    return out.transpose(1, 0, 2, 3, 4).reshape(n, s, N_HEADS, V_DIM)


def layer(x, p_i, pos, conv_prev, past_c, past_kr, w):
    (g_mix_pre, w_in, w_conv, g_q, w_uq, g_kv, w_ukv, g_conv_out, g_attn_out, w_o,
     g_mix_post, g_ffn_pre, w_gate, w_up, w_down, g_ffn_post, w_ple_proj, w_ple_gate) = w
    n, l, _ = x.shape
    h = rms_norm(x, g_mix_pre)
    z = h @ w_in
    gb = z[..., OFF_B:OFF_C]
    gc = z[..., OFF_C:OFF_X]
    xin = z[..., OFF_X:OFF_Q]
    cq = z[..., OFF_Q:OFF_KV]
    ckv_raw = z[..., OFF_KV:OFF_KR]
    kr_raw = z[..., OFF_KR:IN_COLS]

    u = gc * xin
    ext = jnp.concatenate([conv_prev.astype(u.dtype), u], axis=1)
    conv = ext[:, 0:l] * w_conv[0]
    for j in range(1, CONV_W):
        conv = conv + ext[:, j:j + l] * w_conv[j]
    conv_out = gb * conv
    new_conv = ext[:, l:]

    q = (rms_norm(cq, g_q) @ w_uq).reshape(n, l, N_HEADS, QK_DIM)
    q_nope = q[..., :QK_NOPE]
    q_rope = apply_rope(q[..., QK_NOPE:], pos)
    c_kv = rms_norm(ckv_raw, g_kv)
    k_rope = apply_rope(kr_raw, pos)
    if past_c is None:
        kv = (c_kv @ w_ukv).reshape(n, l, N_HEADS, QK_NOPE + V_DIM)
        attn = prompt_attention(q_nope, q_rope, kv[..., :QK_NOPE], k_rope, kv[..., QK_NOPE:])
    else:
        c_all = jnp.concatenate([past_c.astype(c_kv.dtype), c_kv], axis=1)
        kr_all = jnp.concatenate([past_kr.astype(k_rope.dtype), k_rope], axis=1)
        kv = (c_all @ w_ukv).reshape(n, c_all.shape[1], N_HEADS, QK_NOPE + V_DIM)
        attn = attend(q_nope, q_rope, kv[..., :QK_NOPE], kr_all, kv[..., QK_NOPE:], None)
    attn = attn.reshape(n, l, D_ATTN)

    mix = jnp.concatenate([rms_norm(conv_out, g_conv_out), rms_norm(attn, g_attn_out)], axis=-1) @ w_o
    x = x + rms_norm(mix, g_mix_post)

    h = rms_norm(x, g_ffn_pre)
    f = (jax.nn.silu(h @ w_gate) * (h @ w_up)) @ w_down
    x = x + rms_norm(f, g_ffn_post)

    x = x + jax.nn.sigmoid(x @ w_ple_gate) * (p_i @ w_ple_proj)
    return x, c_kv, k_rope, new_conv


def setup_inputs(seed: int = 0) -> dict:
    key = jax.random.key(seed)
    ks = iter(jax.random.split(key, 40))

    def nrm(shape, scale=1.0):
        return jax.random.normal(next(ks), shape, jnp.float32) * scale

    def gain(shape):
        return 1.0 + 0.05 * nrm(shape)

    return {
        "x_prompt": nrm((BATCH, SEQ, D_MODEL)),
        "x_sample": nrm((DEC_BATCH, DEC_SEQ, D_MODEL)),
        "cache_kv_latent": nrm((DEPTH, DEC_BATCH, PAST_LEN, KV_LORA)),
        "cache_k_rope": nrm((DEPTH, DEC_BATCH, PAST_LEN, QK_ROPE)),
        "state_conv": nrm((DEPTH, DEC_BATCH, CONV_W - 1, D_CONV)),
        "p_prompt": nrm((DEPTH, BATCH, SEQ, D_PLE)),
        "p_sample": nrm((DEPTH, DEC_BATCH, DEC_SEQ, D_PLE)),
        "g_mix_pre": gain((DEPTH, D_MODEL)),
        "w_in": nrm((DEPTH, D_MODEL, IN_COLS), D_MODEL ** -0.5),
        "w_conv": nrm((DEPTH, CONV_W, D_CONV), CONV_W ** -0.5),
        "g_q": gain((DEPTH, Q_LORA)),
        "w_uq": nrm((DEPTH, Q_LORA, N_HEADS * QK_DIM), Q_LORA ** -0.5),
        "g_kv": gain((DEPTH, KV_LORA)),
        "w_ukv": nrm((DEPTH, KV_LORA, N_HEADS * (QK_NOPE + V_DIM)), KV_LORA ** -0.5),
        "g_conv_out": gain((DEPTH, D_CONV)),
        "g_attn_out": gain((DEPTH, D_ATTN)),
        "w_o": nrm((DEPTH, D_MIX, D_MODEL), D_MIX ** -0.5),
        "g_mix_post": gain((DEPTH, D_MODEL)),
        "g_ffn_pre": gain((DEPTH, D_MODEL)),
        "w_ffn_gate": nrm((DEPTH, D_MODEL, D_FF), D_MODEL ** -0.5),
        "w_ffn_up": nrm((DEPTH, D_MODEL, D_FF), D_MODEL ** -0.5),
        "w_ffn_down": nrm((DEPTH, D_FF, D_MODEL), D_FF ** -0.5),
        "g_ffn_post": gain((DEPTH, D_MODEL)),
        "w_ple_proj": nrm((DEPTH, D_PLE, D_MODEL), D_PLE ** -0.5),
        "w_ple_gate": nrm((DEPTH, D_MODEL, D_MODEL), D_MODEL ** -0.5),
    }


def reference(x_prompt, x_sample, cache_kv_latent, cache_k_rope, state_conv, p_prompt, p_sample,
              g_mix_pre, w_in, w_conv, g_q, w_uq, g_kv, w_ukv, g_conv_out, g_attn_out, w_o,
              g_mix_post, g_ffn_pre, w_ffn_gate, w_ffn_up, w_ffn_down, g_ffn_post,
              w_ple_proj, w_ple_gate):
    n_p, s_p = x_prompt.shape[0], x_prompt.shape[1]
    s_d = x_sample.shape[1]
    past_len = cache_kv_latent.shape[2]
    pos_p = jnp.arange(s_p)
    pos_d = past_len + jnp.arange(s_d)
    conv_zero = jnp.zeros((n_p, CONV_W - 1, D_CONV), x_prompt.dtype)

    xp, xd = x_prompt, x_sample
    lat_p, kr_p, cv_p, lat_d, kr_d, cv_d = [], [], [], [], [], []
    for i in range(DEPTH):
        w = (g_mix_pre[i], w_in[i], w_conv[i], g_q[i], w_uq[i], g_kv[i], w_ukv[i],
             g_conv_out[i], g_attn_out[i], w_o[i], g_mix_post[i], g_ffn_pre[i],
             w_ffn_gate[i], w_ffn_up[i], w_ffn_down[i], g_ffn_post[i],
             w_ple_proj[i], w_ple_gate[i])
        xp, c_p, k_p, s_p_new = layer(xp, p_prompt[i], pos_p, conv_zero, None, None, w)
        xd, c_d, k_d, s_d_new = layer(xd, p_sample[i], pos_d, state_conv[i],
                                      cache_kv_latent[i], cache_k_rope[i], w)
        lat_p.append(c_p); kr_p.append(k_p); cv_p.append(s_p_new)
        lat_d.append(c_d); kr_d.append(k_d); cv_d.append(s_d_new)

    new_kv_latent_prompt = jnp.stack(lat_p)
    new_k_rope_prompt = jnp.stack(kr_p)
    new_conv_prompt = jnp.stack(cv_p)
    new_kv_latent_sample = jnp.stack(lat_d)
    new_k_rope_sample = jnp.stack(kr_d)
    new_conv_sample = jnp.stack(cv_d)
    return (xp, xd, new_kv_latent_prompt, new_k_rope_prompt, new_conv_prompt,
            new_kv_latent_sample, new_k_rope_sample, new_conv_sample)
```

```python
import numpy as np
import contextlib
import concourse.bass as bass
import concourse.mybir as mybir
from concourse.bass_utils import run_bass_kernel_spmd

F32 = mybir.dt.float32
BF16 = mybir.dt.bfloat16
ALU = mybir.AluOpType
AF = mybir.ActivationFunctionType
AX = mybir.AxisListType

SEM_LIMIT = 30000
N_DMA_SEMS = 24


class Tile:
    def __init__(self, name, nblocks=1):
        self.name = name
        self.nb = nblocks
        self.w = [None] * nblocks
        self.r = [dict() for _ in range(nblocks)]

    def all(self):
        return (self, 0, self.nb)

    def b(self, i, n=1):
        assert 0 <= i and i + n <= self.nb, (self.name, i, n, self.nb)
        return (self, i, i + n)

    def events(self):
        evs = []
        for b in range(self.nb):
            if self.w[b] is not None:
                evs.append(self.w[b])
            evs.extend(self.r[b].values())
        return evs


class Sched:
    def __init__(self, nc, es, same_engine_sync=True):
        self.nc = nc
        self.es = es
        self.eng = {"pe": nc.tensor, "act": nc.scalar, "dve": nc.vector,
                    "pool": nc.gpsimd, "sp": nc.sync}
        self.same_sync = same_engine_sync
        self.nkeys = 0
        self.cur = {}
        self.seen = {e: {} for e in self.eng}
        self.ninstr = {e: 0 for e in self.eng}
        self.nwait = {e: 0 for e in self.eng}
        for e in self.eng:
            self._new_sem(e)
        self.dma_sems = {"hw": [], "sw": []}
        for kind in ("hw", "sw"):
            for i in range(N_DMA_SEMS // 2):
                h = es.enter_context(nc.semaphore("dsem_%s%d" % (kind, i)))
                self.dma_sems[kind].append((self._key(), h, 0))
        self.dma_rr = {"hw": 0, "sw": 0}

    def _key(self):
        self.nkeys += 1
        return self.nkeys

    def _new_sem(self, e):
        h = self.es.enter_context(self.nc.semaphore("s_%s_%d" % (e, self.nkeys)))
        self.cur[e] = [self._key(), h, 0]

    def _collect(self, e, reads, writes, extra=()):
        need = {}
        own = self.cur[e][0]
        seen = self.seen[e]

        def req(ev):
            if ev is None:
                return
            k, h, v = ev
            if k == own and (e == "pe" or not self.same_sync):
                return
            if seen.get(k, 0) >= v:
                return
            if k not in need or need[k][2] < v:
                need[k] = ev

        for (t, lo, hi) in reads:
            for b in range(lo, hi):
                req(t.w[b])
                if getattr(t, "excl", False):
                    for ev in t.r[b].values():
                        if ev[0] != own:
                            req(ev)
        for (t, lo, hi) in writes:
            for b in range(lo, hi):
                req(t.w[b])
                for ev in t.r[b].values():
                    req(ev)
        for ev in extra:
            req(ev)
        for k, (kk, h, v) in need.items():
            self.eng[e].wait_ge(h, v)
            seen[k] = v
            self.nwait[e] += 1

    def _update(self, ev, reads, writes):
        for (t, lo, hi) in writes:
            for b in range(lo, hi):
                t.w[b] = ev
                t.r[b] = {}
        for (t, lo, hi) in reads:
            for b in range(lo, hi):
                if t.w[b] is ev:
                    continue
                t.r[b][ev[0]] = ev

    def op(self, e, fn, reads=(), writes=()):
        self._collect(e, reads, writes)
        ins = fn()
        c = self.cur[e]
        c[2] += 1
        ev = (c[0], c[1], c[2])
        ins.then_inc(c[1], 1)
        self.ninstr[e] += 1
        self._update(ev, reads, writes)
        if c[2] >= SEM_LIMIT:
            self._new_sem(e)
        return ev

    def dma(self, q, out, in_, reads=(), writes=(), **kw):
        kind = "sw" if q == "pool" else "hw"
        pool_ = self.dma_sems[kind]
        slot = self.dma_rr[kind]
        self.dma_rr[kind] = (slot + 1) % len(pool_)
        k, h, v = pool_[slot]
        extra = [(k, h, v)] if v > 0 else []
        self._collect(q, reads, writes, extra)
        ins = self.eng[q].dma_start(out=out, in_=in_, **kw)
        ins.then_inc(h, 16)
        ev = (k, h, v + 16)
        pool_[slot] = ev
        self.ninstr[q] += 1
        self._update(ev, reads, writes)
        return ev

    def finish(self):
        for (k, h, v) in self.dma_sems["hw"] + self.dma_sems["sw"]:
            if v > 0 and self.seen["sp"].get(k, 0) < v:
                self.nc.sync.wait_ge(h, v)
                self.seen["sp"][k] = v


class Buf(Tile):
    def __init__(self, arena, name, off, nwords, nblocks):
        super().__init__(name, nblocks)
        self.arena = arena
        self.off = off
        self.nwords = nwords

    def f32(self, p0=0, p1=128):
        return self.arena.ap[p0:p1, self.off:self.off + self.nwords]

    def bf(self, p0=0, p1=128):
        return self.arena.ap[p0:p1, self.off:self.off + self.nwords].bitcast(BF16)


class Arena:
    def __init__(self, ap, nwords):
        self.ap = ap
        self.nwords = nwords
        self.free_list = [(0, nwords)]
        self.ghosts = []
        self.peak = 0
        self.live = {}

    def alloc(self, name, nbytes, nblocks=1):
        nw = (nbytes + 3) // 4
        nw = (nw + 7) // 8 * 8
        for i, (o, n) in enumerate(self.free_list):
            if n >= nw:
                if n == nw:
                    self.free_list.pop(i)
                else:
                    self.free_list[i] = (o + nw, n - nw)
                b = Buf(self, name, o, nw, nblocks)
                self.peak = max(self.peak, o + nw)
                inh = {}
                keep = []
                for (go, ge, evs) in self.ghosts:
                    if go < o + nw and ge > o:
                        for ev in evs:
                            if ev[0] not in inh or inh[ev[0]][2] < ev[2]:
                                inh[ev[0]] = ev
                        if go >= o and ge <= o + nw:
                            continue
                    keep.append((go, ge, evs))
                self.ghosts = keep
                for blk in range(nblocks):
                    b.r[blk] = dict(inh)
                self.live[id(b)] = b
                return b
        raise RuntimeError("arena OOM allocating %s (%d bytes); free=%s" % (name, nbytes, self.free_list))

    def free(self, b):
        del self.live[id(b)]
        evs = {}
        for ev in b.events():
            if ev[0] not in evs or evs[ev[0]][2] < ev[2]:
                evs[ev[0]] = ev
        self.ghosts.append((b.off, b.off + b.nwords, list(evs.values())))
        fl = self.free_list + [(b.off, b.nwords)]
        fl.sort()
        out = []
        for (o, n) in fl:
            if out and out[-1][0] + out[-1][1] == o:
                out[-1] = (out[-1][0], out[-1][1] + n)
            else:
                out.append((o, n))
        self.free_list = out


class PsumPool:
    def __init__(self, nc, es, nbanks=8):
        self.banks = []
        for i in range(nbanks):
            t = es.enter_context(nc.psum_tensor("psb%d" % i, [128, 512], F32))
            tl = Tile("psb%d" % i, 1)
            tl.excl = True
            tl.t = t
            self.banks.append(tl)
        self.free = list(self.banks)

    def get(self):
        assert self.free, "PSUM pool exhausted"
        return self.free.pop(0)

    def put(self, b):
        self.free.append(b)

EPS = 1e-6
SCALE = 96 ** -0.5
DFF = 2816
NFC = 22


_NO_SAMPLE = False


class _Stop(Exception):
    pass


def build_program(NSEQ, T, PAST, stop_at=None):
    nc = bass.Bass("TRN2", target_bir_lowering=False)
    NTG = T // 512
    NTT = T // 128
    NS = 32

    def din(name, shape):
        return nc.dram_tensor(name, list(shape), F32, kind="ExternalInput").ap()

    def dout(name, shape):
        return nc.dram_tensor(name, list(shape), F32, kind="ExternalOutput").ap()

    xp = din("xp", [NSEQ, T, 1024]); xs = din("xs", [NS, 1024])
    ckvp = din("ckvp", [2, PAST, 256]); krp = din("krp", [2, PAST, 32]); cst = din("cst", [2, 2, 512])
    pp = din("pp", [2, NSEQ, T, 256]); psm = din("psm", [2, NS, 256])
    g_mix_pre = din("g_mix_pre", [2, 1024]); w_in = din("w_in", [2, 1024, 2592]); w_conv = din("w_conv", [2, 3, 512])
    g_q = din("g_q", [2, 768]); w_uq = din("w_uq", [2, 768, 768]); g_kv = din("g_kv", [2, 256])
    w_ukv = din("w_ukv", [2, 256, 1024]); g_conv_out = din("g_conv_out", [2, 512]); g_attn_out = din("g_attn_out", [2, 512])
    w_o = din("w_o", [2, 1024, 1024]); g_mix_post = din("g_mix_post", [2, 1024]); g_ffn_pre = din("g_ffn_pre", [2, 1024])
    w_gate = din("w_ffn_gate", [2, 1024, DFF]); w_up = din("w_ffn_up", [2, 1024, DFF]); w_down = din("w_ffn_down", [2, DFF, 1024])
    g_ffn_post = din("g_ffn_post", [2, 1024]); w_pproj = din("w_ple_proj", [2, 256, 1024]); w_pgate = din("w_ple_gate", [2, 1024, 1024])
    identd = din("identd", [128, 128])
    c4p = din("c4p", [128, T]); s4p = din("s4p", [128, T]); c4s = din("c4s", [128, NS]); s4s = din("s4s", [128, NS])
    ctp = din("ctp", [T, 16]); stp = din("stp", [T, 16]); cts = din("cts", [NS, 16]); sts = din("sts", [NS, 16])

    y_p = dout("y_p", [NSEQ, T, 1024]); y_s = dout("y_s", [NS, 1024])
    lat_p = dout("lat_p", [2, NSEQ, T, 256]); kr_p = dout("kr_p", [2, NSEQ, T, 32]); conv_p = dout("conv_p", [2, NSEQ, 2, 512])
    lat_s = dout("lat_s", [2, NS, 256]); kr_s = dout("kr_s", [2, NS, 32]); conv_s = dout("conv_s", [2, 2, 512])

    es = contextlib.ExitStack()
    with es, nc.allow_low_precision("bf16 matmul operands, fp32 accumulation"), \
            nc.allow_non_contiguous_dma("small strided parameter / state transfers"):
        S = Sched(nc, es)
        NW = 212800 // 4
        ar_t = es.enter_context(nc.sbuf_tensor("arena", [128, NW], F32))
        AR = Arena(ar_t, NW)
        PS = PsumPool(nc, es)
        V = nc.vector; A = nc.scalar; G = nc.gpsimd; PE = nc.tensor

        def mm(out, lhsT, rhs, start, stop, reads, writes):
            S.op("pe", lambda: PE.matmul(out, lhsT=lhsT, rhs=rhs, start=start, stop=stop, skip_group_check=True),
                 reads, writes)

        def v3(ap, a):
            return ap.rearrange("p (a b) -> p a b", a=a)

        ident = AR.alloc("ident", 128 * 4); identb = AR.alloc("identb", 128 * 2); ones = AR.alloc("ones", 128 * 2)
        epsb = AR.alloc("epsb", 4)
        S.dma("sp", ident.f32(), identd[:, :], writes=[ident.all()])
        S.dma("pool", identb.bf(), identd[:, :], writes=[identb.all()])
        S.op("dve", lambda: V.memset(ones.bf(), 1.0), writes=[ones.all()])
        S.op("dve", lambda: V.memset(epsb.f32(), EPS), writes=[epsb.all()])
        gmp = AR.alloc("gmp", 16 * 4); gq = AR.alloc("gq", 12 * 4); gco = AR.alloc("gco", 8 * 4); gao = AR.alloc("gao", 8 * 4)
        gpo = AR.alloc("gpo", 16 * 4); gfp = AR.alloc("gfp", 16 * 4); gfo = AR.alloc("gfo", 16 * 4); wcv = AR.alloc("wcv", 24 * 4)
        gkvb = AR.alloc("gkvb", 512 * 4)
        for (buf, src, nchunk) in ((gmp, g_mix_pre, 8), (gq, g_q, 6), (gco, g_conv_out, 4), (gao, g_attn_out, 4),
                                   (gpo, g_mix_post, 8), (gfp, g_ffn_pre, 8), (gfo, g_ffn_post, 8)):
            for l in range(2):
                S.dma("sp", buf.f32()[:, l * nchunk:(l + 1) * nchunk], src[l].rearrange("(c p) -> p c", p=128),
                      writes=[buf.all()])
        for l in range(2):
            for j in range(3):
                o = (l * 3 + j) * 4
                S.dma("sp", wcv.f32()[:, o:o + 4], w_conv[l, j].rearrange("(c p) -> p c", p=128), writes=[wcv.all()])
            S.dma("sp", gkvb.f32()[:, l * 256:(l + 1) * 256], g_kv[l].partition_broadcast(128), writes=[gkvb.all()])
        ctk = AR.alloc("ctk", NTT * 16 * 4); stk = AR.alloc("stk", NTT * 16 * 4)
        S.dma("sp", v3(ctk.f32(), NTT), ctp.rearrange("(a p) f -> p a f", p=128), writes=[ctk.all()])
        S.dma("sp", v3(stk.f32(), NTT), stp.rearrange("(a p) f -> p a f", p=128), writes=[stk.all()])
        ctks = AR.alloc("ctks", 16 * 4); stks = AR.alloc("stks", 16 * 4)
        S.dma("sp", ctks.f32(0, NS), cts[:, :], writes=[ctks.all()])
        S.dma("sp", stks.f32(0, NS), sts[:, :], writes=[stks.all()])

        xT = AR.alloc("xT", 8 * T * 4, NTG); xTv = v3(xT.f32(), 8)
        xsT = AR.alloc("xsT", 8 * NS * 4, 1); xsTv = v3(xsT.f32(), 8)
        uh = AR.alloc("uh", 8 * 4, 1); uhv = v3(uh.f32(), 4)
        class TG:
            pass

        def make_tgs(s, cvnv, cvn, cvnsv, cvns):
            tgs = []
            for g in range(NTG):
                t = TG(); t.sample = False; t.g = g; t.n = 512; t.t0 = g * 512; t.s = s
                t.x = xTv[:, :, t.t0:t.t0 + 512]; t.xd = xT.b(g)
                t.cvn = cvnv[:, :, t.t0:t.t0 + 512]; t.cvnd = cvn.b(g)
                t.ntile = 4; t.tp = 128
                tgs.append(t)
            if s == 0 and not _NO_SAMPLE:
                t = TG(); t.sample = True; t.g = NTG; t.n = NS; t.t0 = 0; t.s = 0
                t.x = xsTv; t.xd = xsT.all(); t.cvn = cvnsv; t.cvnd = cvns.all(); t.ntile = 1; t.tp = NS
                tgs.append(t)
            return tgs

        def rstd_from_chunks(chunks, D, n, name):
            bank = PS.get()
            sqs = [AR.alloc(name + "_sq%d" % i, n * 2) for i in range(2)]
            sc = float(D) ** -0.5
            for i, (ap, rd) in enumerate(chunks):
                sq = sqs[i % 2]
                S.op("act", lambda ap=ap, sq=sq: A.activation(out=sq.bf(), in_=ap, func=AF.Square, scale=sc),
                     reads=rd, writes=[sq.all()])
                mm(bank.t[:, 0:n], ones.bf(), sq.bf(), i == 0, i == len(chunks) - 1, [ones.all(), sq.all()], [bank.all()])
            rb = AR.alloc(name + "_rb", n * 4)
            S.op("act", lambda: A.activation(out=rb.f32(), in_=bank.t[:, 0:n], func=AF.Ln, bias=epsb.f32()[:, 0:1], scale=1.0),
                 reads=[bank.all(), epsb.all()], writes=[rb.all()])
            S.op("act", lambda: A.activation(out=rb.f32(), in_=rb.f32(), func=AF.Exp, scale=-0.5), reads=[rb.all()], writes=[rb.all()])
            PS.put(bank)
            for q in sqs:
                AR.free(q)
            return rb

        def make_hT(tg, gbuf, l, name, dst=None):
            n = tg.n
            if dst is None:
                hT = AR.alloc(name + "_hT", 8 * n * 2, 8)
                hv = v3(hT.bf(), 8)
                boff = 0
            else:
                hT, hv, boff = dst
            rb = rstd_from_chunks([(tg.x[:, c, :], [tg.xd]) for c in range(8)], 1024, n, name)
            for c in range(8):
                S.op("dve", lambda c=c: V.scalar_tensor_tensor(out=hv[:, c, :], in0=tg.x[:, c, :], scalar=gbuf.f32()[:, l * 8 + c:l * 8 + c + 1],
                                                               in1=rb.f32(), op0=ALU.mult, op1=ALU.mult),
                     reads=[tg.xd, gbuf.all(), rb.all()], writes=[hT.b(boff + c)])
            AR.free(rb)
            return hT, hv

        def load_w(name, nbytes, dmas):
            b = AR.alloc(name, nbytes)
            for (ov, iv) in dmas:
                S.dma("pool", ov(b), iv, writes=[b.all()])
            return b

        def kv_from_tokmajor(ckv_bf_tiles, kr_bf_tiles, ntile, tp, l, wuk, wuv, KT_dst, KT_dep, V_dst_fn, V_dep):
            n = ntile * tp
            bT = PS.get(); bR = PS.get()
            bTb = bT.t[:, :].bitcast(BF16)
            bRb = bR.t[:, :].bitcast(BF16)
            for i in range(ntile):
                (cap, cdep) = ckv_bf_tiles[i]
                (kap, kdep) = kr_bf_tiles[i]
                for c in range(2):
                    S.op("pe", lambda c=c, i=i, cap=cap: PE.transpose(bTb[:, c * 512 + i * tp:c * 512 + (i + 1) * tp], cap[:, c * 128:(c + 1) * 128],
                                                                     identb.bf()[0:tp, 0:tp]),
                         reads=cdep + [identb.all()], writes=[bT.all()])
                S.op("pe", lambda i=i, kap=kap: PE.transpose(bRb[0:32, i * tp:(i + 1) * tp], kap, identb.bf()[0:tp, 0:tp]),
                     reads=kdep + [identb.all()], writes=[bR.all()])
            ckvT = AR.alloc("ckvT", 2 * n * 2)
            cTv = v3(ckvT.bf(), 2)
            for c in range(2):
                S.op("act", lambda c=c: A.activation(out=cTv[:, c, :], in_=bTb[:, c * 512:c * 512 + n], func=AF.Copy),
                     reads=[bT.all()], writes=[ckvT.all()])
            krs = AR.alloc("krs", n * 2)
            S.op("dve", lambda: V.tensor_copy(out=krs.bf(0, 32)[:, 0:n], in_=bRb[0:32, 0:n]), reads=[bR.all()], writes=[krs.all()])
            PS.put(bT); PS.put(bR)
            for h in range(8):
                if h % 2 == 0:
                    S.op("dve", lambda h=h: V.tensor_copy(out=KT_dst[64:96, h, :], in_=krs.bf(0, 32)[:, 0:n]), reads=[krs.all()], writes=KT_dep)
                else:
                    S.op("act", lambda h=h: A.activation(out=KT_dst[64:96, h, :], in_=krs.bf(0, 32)[:, 0:n], func=AF.Copy), reads=[krs.all()], writes=KT_dep)
            for pr in range(4):
                bk = PS.get()
                for c in range(2):
                    mm(bk.t[:, 0:n], wuk[:, c, pr * 128:(pr + 1) * 128], cTv[:, c, :], c == 0, c == 1, [wuk_b.all(), ckvT.all()], [bk.all()])
                S.op("act", lambda pr=pr, bk=bk: A.activation(out=KT_dst[0:64, 2 * pr, :], in_=bk.t[0:64, 0:n], func=AF.Copy), reads=[bk.all()], writes=KT_dep)
                S.op("dve", lambda pr=pr, bk=bk: V.tensor_copy(out=KT_dst[0:64, 2 * pr + 1, :], in_=bk.t[64:128, 0:n]), reads=[bk.all()], writes=KT_dep)
                PS.put(bk)
            for i in range(ntile):
                bv = PS.get()
                for c in range(2):
                    mm(bv.t[0:tp, :], cTv[:, c, i * tp:(i + 1) * tp], wuv[:, c, :], c == 0, c == 1, [wuv_b.all(), ckvT.all()], [bv.all()])
                vd = V_dst_fn(i)
                if i % 2 == 0:
                    S.op("dve", lambda vd=vd, bv=bv: V.tensor_copy(out=vd, in_=bv.t[0:tp, :]), reads=[bv.all()], writes=V_dep)
                else:
                    S.op("act", lambda vd=vd, bv=bv: A.activation(out=vd, in_=bv.t[0:tp, :], func=AF.Copy), reads=[bv.all()], writes=V_dep)
                PS.put(bv)
            AR.free(ckvT); AR.free(krs)

        wuk_b = None; wuv_b = None

        def ckpt(i):
            if stop_at is not None and i == stop_at:
                raise _Stop()

        pending_A = [None]

        def load_A_weights(l_):
            w_in_v_ = w_in[l_].rearrange("(k p) m -> p k m", p=128)
            w_ukv_v_ = w_ukv[l_].rearrange("(k p) (h e d) -> p k h e d", p=128, h=8, e=2)
            wkv_b_ = load_w("wkv", 8 * 288 * 2, [(lambda b: v3(b.bf(), 8), w_in_v_[:, :, 2304:2592])])
            wuk_b_ = load_w("wuk", 2 * 512 * 2, [((lambda b, k=k: b.bf().rearrange("p (k h d) -> p k h d", k=2, h=8)[:, k, :, :]), w_ukv_v_[:, k, :, 0, :]) for k in range(2)])
            wuv_b_ = load_w("wuv", 2 * 512 * 2, [((lambda b, k=k: b.bf().rearrange("p (k h d) -> p k h d", k=2, h=8)[:, k, :, :]), w_ukv_v_[:, k, :, 1, :]) for k in range(2)])
            wcv_b_ = AR.alloc("wconv", 8 * 1536 * 2, 12)
            wv_ = v3(wcv_b_.bf(), 8)
            for j in range(4):
                for kind in (2, 1, 0):
                    c0 = kind * 512 + j * 128
                    S.dma("pool", wv_[:, :, c0:c0 + 128], w_in_v_[:, :, c0:c0 + 128], writes=[wcv_b_.b(kind * 4 + j)])
            return (wcv_b_, wkv_b_, wuk_b_, wuv_b_)

        try:
            for s in range(NSEQ):
                xin = [AR.alloc("xin%d" % i, 1024 * 4) for i in range(2)]

                def load_x(src_ap, tp, dstv, dep, idx):
                    xb = xin[idx % 2]
                    S.dma("sp", xb.f32(0, tp), src_ap, writes=[xb.all()])
                    for half in range(2):
                        bk = PS.get()
                        for cc in range(4):
                            c = half * 4 + cc
                            S.op("pe", lambda c=c, cc=cc, bk=bk, xb=xb: PE.transpose(bk.t[:, cc * 128:cc * 128 + tp], xb.f32(0, tp)[:, c * 128:(c + 1) * 128],
                                                                                ident.f32()[0:tp, 0:tp]),
                                 reads=[xb.all(), ident.all()], writes=[bk.all()])
                        src = bk.t[:, :].rearrange("p (a b) -> p a b", a=4)[:, :, 0:tp]
                        if half == 0:
                            S.op("dve", lambda src=src: V.tensor_copy(out=dstv[:, 0:4, :], in_=src), reads=[bk.all()], writes=dep)
                        else:
                            S.op("act", lambda src=src: A.activation(out=dstv[:, 4:8, :], in_=src, func=AF.Copy), reads=[bk.all()], writes=dep)
                        PS.put(bk)

                for tt in range(NTT):
                    load_x(xp[s, tt * 128:(tt + 1) * 128, :], 128, xTv[:, :, tt * 128:(tt + 1) * 128], [xT.b(tt // 4)], tt)
                if s == 0:
                    load_x(xs[:, :], NS, xsTv, [xsT.all()], 0)
                for b in xin:
                    AR.free(b)
                ckpt(0)

                for l in range(2):
                    KT = AR.alloc("KT", 8 * T * 2, NTG); KTv = v3(KT.bf(), 8)
                    Vt = AR.alloc("Vt", NTT * 512 * 2, NTG); Vtv = v3(Vt.bf(), NTT)
                    cvn = AR.alloc("cvn", 4 * T * 2, NTG); cvnv = v3(cvn.bf(), 4)
                    cvns = AR.alloc("cvns", 4 * NS * 2, 1); cvnsv = v3(cvns.bf(), 4)
                    KTs = AR.alloc("KTs", 8 * NS * 2, 1); KTsv = v3(KTs.bf(), 8)
                    Vts = AR.alloc("Vts", 512 * 2, 1)
                    tgs = make_tgs(s, cvnv, cvn, cvnsv, cvns)
                    ptgs = tgs[:NTG]
                    w_in_v = w_in[l].rearrange("(k p) m -> p k m", p=128)
                    if pending_A[0] is not None:
                        wcv_b, wkv_b, wuk_b, wuv_b = pending_A[0]
                        pending_A[0] = None
                    else:
                        wcv_b, wkv_b, wuk_b, wuv_b = load_A_weights(l)
                    wcvv = v3(wcv_b.bf(), 8)
                    wkvv = v3(wkv_b.bf(), 8)
                    w_ukv_v = w_ukv[l].rearrange("(k p) (h e d) -> p k h e d", p=128, h=8, e=2)
                    wuk = v3(wuk_b.bf(), 2); wuv = v3(wuv_b.bf(), 2)

                    nxt_h = make_hT(tgs[0], gmp, l, "a")
                    for tgi, tg in enumerate(tgs):
                        n = tg.n
                        hT, hv = nxt_h
                        tp = tg.tp; nt = tg.ntile
                        cbs = []; kbs = []; tmp_bufs = []
                        krr = AR.alloc("krr", nt * 32 * 4); kro = AR.alloc("kro", nt * 32 * 4); kt = AR.alloc("kt", nt * 32 * 4); kbf = AR.alloc("kbf", nt * 32 * 2)
                        krrv = v3(krr.f32(0, tp), nt); krov = v3(kro.f32(0, tp), nt); ktv = v3(kt.f32(0, tp), nt); kbfv = v3(kbf.bf(0, tp), nt)
                        ssb = AR.alloc("ssb", 8 * 4); junk = AR.alloc("junk", 256 * 2)
                        S.op("pool", lambda: G.memset(ssb.f32(), 0.0), writes=[ssb.all()])
                        bkvs = []
                        for i in range(nt):
                            bkv = PS.get(); bkvs.append(bkv)
                            for k in range(8):
                                mm(bkv.t[0:tp, 0:288], hv[:, k, i * tp:(i + 1) * tp], wkvv[:, k, :], k == 0, k == 7, [hT.b(k), wkv_b.all()], [bkv.all()])
                            S.op("act", lambda bkv=bkv, i=i: A.activation(out=junk.bf(0, tp), in_=bkv.t[0:tp, 0:256], func=AF.Square, scale=1.0 / 16.0,
                                                                          accum_out=ssb.f32(0, tp)[:, i:i + 1]),
                                 reads=[bkv.all()], writes=[junk.all(), ssb.all()])
                            S.op("act", lambda bkv=bkv, i=i: A.activation(out=krrv[:, i, :], in_=bkv.t[0:tp, 256:288], func=AF.Copy), reads=[bkv.all()], writes=[krr.all()])
                        S.op("act", lambda: A.activation(out=ssb.f32(0, tp)[:, 0:nt], in_=ssb.f32(0, tp)[:, 0:nt], func=AF.Ln, bias=epsb.f32(0, tp)[:, 0:1], scale=1.0),
                             reads=[ssb.all(), epsb.all()], writes=[ssb.all()])
                        S.op("act", lambda: A.activation(out=ssb.f32(0, tp)[:, 0:nt], in_=ssb.f32(0, tp)[:, 0:nt], func=AF.Exp, scale=-0.5), reads=[ssb.all()], writes=[ssb.all()])
                        for i in range(nt):
                            bkv = bkvs[i]
                            ckv = AR.alloc("ckv", 256 * 4); cbf = AR.alloc("cbf", 256 * 2)
                            S.op("dve", lambda bkv=bkv, ckv=ckv, i=i: V.scalar_tensor_tensor(out=ckv.f32(0, tp), in0=bkv.t[0:tp, 0:256], scalar=ssb.f32(0, tp)[:, i:i + 1],
                                                                                             in1=gkvb.f32(0, tp)[:, l * 256:(l + 1) * 256], op0=ALU.mult, op1=ALU.mult),
                                 reads=[bkv.all(), ssb.all(), gkvb.all()], writes=[ckv.all()])
                            PS.put(bkv)
                            S.op("act", lambda ckv=ckv, cbf=cbf: A.activation(out=cbf.bf(0, tp), in_=ckv.f32(0, tp), func=AF.Copy), reads=[ckv.all()], writes=[cbf.all()])
                            if tg.sample:
                                S.dma("sp", lat_s[l][:, :], ckv.f32(0, tp), reads=[ckv.all()])
                            else:
                                r0 = tg.t0 + i * 128
                                S.dma("sp", lat_p[l, tg.s, r0:r0 + 128, :], ckv.f32(), reads=[ckv.all()])
                            cbs.append((cbf.bf(0, tp), [cbf.all()])); kbs.append((kbfv[:, i, :], [kbf.all()]))
                            tmp_bufs += [ckv, cbf]
                        if tg.sample:
                            cs = ctks.f32(0, tp).rearrange("p (a f) -> p a f", a=1); sn = stks.f32(0, tp).rearrange("p (a f) -> p a f", a=1)
                            cdep = [ctks.all(), stks.all()]
                        else:
                            cs = v3(ctk.f32(), NTT)[:, tg.g * 4:tg.g * 4 + 4, :]; sn = v3(stk.f32(), NTT)[:, tg.g * 4:tg.g * 4 + 4, :]
                            cdep = [ctk.all(), stk.all()]
                        x1 = krrv[:, :, 0:16]; x2 = krrv[:, :, 16:32]
                        o1 = krov[:, :, 0:16]; o2 = krov[:, :, 16:32]
                        t1_ = ktv[:, :, 0:16]; t2_ = ktv[:, :, 16:32]
                        S.op("pool", lambda: G.tensor_tensor(out=o1, in0=x1, in1=cs, op=ALU.mult), reads=[krr.all()] + cdep, writes=[kro.all()])
                        S.op("pool", lambda: G.tensor_tensor(out=t1_, in0=x2, in1=sn, op=ALU.mult), reads=[krr.all()] + cdep, writes=[kt.all()])
                        S.op("pool", lambda: G.tensor_tensor(out=o1, in0=o1, in1=t1_, op=ALU.subtract), reads=[kro.all(), kt.all()], writes=[kro.all()])
                        S.op("pool", lambda: G.tensor_tensor(out=o2, in0=x1, in1=sn, op=ALU.mult), reads=[krr.all()] + cdep, writes=[kro.all()])
                        S.op("pool", lambda: G.tensor_tensor(out=t2_, in0=x2, in1=cs, op=ALU.mult), reads=[krr.all()] + cdep, writes=[kt.all()])
                        S.op("pool", lambda: G.tensor_tensor(out=o2, in0=o2, in1=t2_, op=ALU.add), reads=[kro.all(), kt.all()], writes=[kro.all()])
                        S.op("pool", lambda: G.tensor_copy(out=kbfv, in_=krov), reads=[kro.all()], writes=[kbf.all()])
                        if tg.sample:
                            S.dma("sp", kr_s[l][:, :], kro.f32(0, tp), reads=[kro.all()])
                        else:
                            S.dma("sp", kr_p[l, tg.s, tg.t0:tg.t0 + 512, :].rearrange("(a p) f -> p a f", p=128), krov, reads=[kro.all()])
                        tmp_bufs += [junk, ssb, krr, kro, kt, kbf]
                        u = AR.alloc("u", 4 * (n + 2) * 4); uv = v3(u.f32(), 4)
                        co = AR.alloc("co", 4 * n * 4); cov = v3(co.f32(), 4)
                        if tg.sample:
                            for t_ in range(2):
                                S.dma("sp", uv[:, :, t_], cst[l, t_].rearrange("(c p) -> p c", p=128), writes=[u.all()])
                        elif tg.g == 0:
                            S.op("pool", lambda: G.memset(uv[:, :, 0:2], 0.0), writes=[u.all()])
                        else:
                            S.op("pool", lambda: G.tensor_copy(out=uv[:, :, 0:2], in_=uhv), reads=[uh.all()], writes=[u.all()])
                        for j in range(4):
                            bx = PS.get(); bc = PS.get(); bb = PS.get()
                            for (bank, off) in ((bx, 1024), (bc, 512), (bb, 0)):
                                for k in range(8):
                                    mm(bank.t[:, 0:n], wcvv[:, k, off + j * 128:off + (j + 1) * 128], hv[:, k, :], k == 0, k == 7,
                                       [wcv_b.b((off // 512) * 4 + j), hT.b(k)], [bank.all()])
                            xsb = AR.alloc("xsb", n * 4)
                            S.op("act", lambda bx=bx, xsb=xsb: A.activation(out=xsb.f32(), in_=bx.t[:, 0:n], func=AF.Copy), reads=[bx.all()], writes=[xsb.all()])
                            PS.put(bx)
                            S.op("dve", lambda j=j, bc=bc, xsb=xsb: V.tensor_tensor(out=uv[:, j, 2:2 + n], in0=bc.t[:, 0:n], in1=xsb.f32(), op=ALU.mult),
                                 reads=[bc.all(), xsb.all()], writes=[u.all()])
                            PS.put(bc); AR.free(xsb)
                            t1 = AR.alloc("t1", n * 4)
                            wo_ = l * 12
                            S.op("act", lambda j=j, t1=t1: A.activation(out=t1.f32(), in_=uv[:, j, 0:n], func=AF.Copy, scale=wcv.f32()[:, wo_ + j:wo_ + j + 1]),
                                 reads=[u.all(), wcv.all()], writes=[t1.all()])
                            S.op("dve", lambda j=j, t1=t1: V.scalar_tensor_tensor(out=t1.f32(), in0=uv[:, j, 1:1 + n], scalar=wcv.f32()[:, wo_ + 4 + j:wo_ + 4 + j + 1],
                                                                                in1=t1.f32(), op0=ALU.mult, op1=ALU.add),
                                 reads=[u.all(), wcv.all(), t1.all()], writes=[t1.all()])
                            S.op("dve", lambda j=j, t1=t1: V.scalar_tensor_tensor(out=t1.f32(), in0=uv[:, j, 2:2 + n], scalar=wcv.f32()[:, wo_ + 8 + j:wo_ + 8 + j + 1],
                                                                                in1=t1.f32(), op0=ALU.mult, op1=ALU.add),
                                 reads=[u.all(), wcv.all(), t1.all()], writes=[t1.all()])
                            S.op("dve", lambda j=j, bb=bb, t1=t1: V.tensor_tensor(out=cov[:, j, :], in0=bb.t[:, 0:n], in1=t1.f32(), op=ALU.mult),
                                 reads=[bb.all(), t1.all()], writes=[co.all()])
                            PS.put(bb); AR.free(t1)
                        if not tg.sample:
                            if tg.g < NTG - 1:
                                S.op("pool", lambda: G.tensor_copy(out=uhv, in_=uv[:, :, n:n + 2]), reads=[u.all()], writes=[uh.all()])
                            else:
                                for t_ in range(2):
                                    S.dma("sp", conv_p[l, tg.s, t_].rearrange("(c p) -> p c", p=128), uv[:, :, n + t_], reads=[u.all()])
                        else:
                            for t_ in range(2):
                                S.dma("sp", conv_s[l, t_].rearrange("(c p) -> p c", p=128), uv[:, :, n + t_], reads=[u.all()])
                        AR.free(hT)
                        if tgi + 1 < len(tgs):
                            nxt_h = make_hT(tgs[tgi + 1], gmp, l, "a")
                        if tg.sample:
                            kv_from_tokmajor(cbs, kbs, 1, NS, l, wuk, wuv, KTsv, [KTs.all()], lambda i: Vts.bf(0, NS), [Vts.all()])
                        else:
                            g = tg.g
                            kv_from_tokmajor(cbs, kbs, 4, 128, l, wuk, wuv, KTv[:, :, tg.t0:tg.t0 + 512], [KT.b(g)],
                                             lambda i, g=g: Vtv[:, g * 4 + i, :], [Vt.b(g)])
                        rc = rstd_from_chunks([(cov[:, j, :], [co.all()]) for j in range(4)], 512, n, "c")
                        for j in range(4):
                            S.op("dve", lambda j=j: V.scalar_tensor_tensor(out=tg.cvn[:, j, :], in0=cov[:, j, :], scalar=gco.f32()[:, l * 4 + j:l * 4 + j + 1],
                                                                           in1=rc.f32(), op0=ALU.mult, op1=ALU.mult),
                                 reads=[co.all(), gco.all(), rc.all()], writes=[tg.cvnd])
                        AR.free(rc); AR.free(u); AR.free(co)
                        ckpt(1)
                        for b in tmp_bufs:
                            AR.free(b)
                        ckpt(2)
                    AR.free(wcv_b); AR.free(wkv_b)
                    AR.free(wuk_b); AR.free(wuv_b)

                    wcq_b = AR.alloc("wcq", 8 * 768 * 2, 6)
                    wcqv = v3(wcq_b.bf(), 8)
                    for j in range(6):
                        S.dma("pool", wcqv[:, :, j * 128:(j + 1) * 128], w_in_v[:, :, 1536 + j * 128:1536 + (j + 1) * 128], writes=[wcq_b.b(j)])
                    w_uq_v = w_uq[l].rearrange("(k p) (h d) -> p k h d", p=128, h=8)
                    wuq_b = load_w("wuq", 6 * 768 * 2, [(lambda b: v3(b.bf(), 6), w_uq[l].rearrange("(k p) m -> p k m", p=128))])
                    wuq4 = wuq_b.bf().rearrange("p (k h d) -> p k h d", k=6, h=8)

                    def q4(b):
                        return b.bf().rearrange("p (k h d) -> p k h d", k=6, h=8)
                    wqn_b = AR.alloc("wqn", 6 * 512 * 2); wqa_b = AR.alloc("wqa", 6 * 256 * 2); wqb_b = AR.alloc("wqb", 6 * 256 * 2)
                    for k in range(6):
                        S.op("pool", lambda k=k: G.tensor_copy(out=q4(wqn_b)[:, k, :, :], in_=wuq4[:, k, :, 0:64]), reads=[wuq_b.all()], writes=[wqn_b.all()])
                        S.op("pool", lambda k=k: G.tensor_copy(out=q4(wqa_b)[:, k, :, :], in_=wuq4[:, k, :, 64:96]), reads=[wuq_b.all()], writes=[wqa_b.all()])
                        S.op("pool", lambda k=k: G.tensor_copy(out=q4(wqb_b)[:, k, :, 0:16], in_=wuq4[:, k, :, 80:96]), reads=[wuq_b.all()], writes=[wqb_b.all()])
                        S.op("pool", lambda k=k: G.tensor_copy(out=q4(wqb_b)[:, k, :, 16:32], in_=wuq4[:, k, :, 64:80]), reads=[wuq_b.all()], writes=[wqb_b.all()])
                    AR.free(wuq_b)
                    wqn = v3(wqn_b.bf(), 6); wqa = v3(wqa_b.bf(), 6); wqb = v3(wqb_b.bf(), 6)
                    wo_b = load_w("wo", 8 * 1024 * 2, [(lambda b: v3(b.bf(), 8), w_o[l].rearrange("(k p) m -> p k m", p=128))])
                    wov = v3(wo_b.bf(), 8)
                    ckpt(30)

                    for tg in tgs:
                        n = tg.n
                        hT, hv = make_hT(tg, gmp, l, "b")
                        cqT = AR.alloc("cqT", 6 * n * 2); cqv = v3(cqT.bf(), 6)
                        bankq = PS.get()
                        sqs = [AR.alloc("cq_sq%d" % i, n * 2) for i in range(2)]
                        for j in range(6):
                            bk = PS.get()
                            for k in range(8):
                                mm(bk.t[:, 0:n], wcqv[:, k, j * 128:(j + 1) * 128], hv[:, k, :], k == 0, k == 7, [wcq_b.b(j), hT.b(k)], [bk.all()])
                            S.op("act", lambda j=j, bk=bk: A.activation(out=cqv[:, j, :], in_=bk.t[:, 0:n], func=AF.Copy, scale=gq.f32()[:, l * 6 + j:l * 6 + j + 1]),
                                 reads=[bk.all(), gq.all()], writes=[cqT.all()])
                            sq = sqs[j % 2]
                            S.op("act", lambda bk=bk, sq=sq: A.activation(out=sq.bf(), in_=bk.t[:, 0:n], func=AF.Square, scale=768.0 ** -0.5), reads=[bk.all()], writes=[sq.all()])
                            PS.put(bk)
                            if j > 0:
                                mm(bankq.t[:, 0:n], ones.bf(), prev_sq.bf(), j == 1, False, [ones.all(), prev_sq.all()], [bankq.all()])
                            prev_sq = sq
                            if j == 5:
                                mm(bankq.t[:, 0:n], ones.bf(), prev_sq.bf(), False, True, [ones.all(), prev_sq.all()], [bankq.all()])
                        rq = AR.alloc("rq", n * 4)
                        S.op("act", lambda: A.activation(out=rq.f32(), in_=bankq.t[:, 0:n], func=AF.Ln, bias=epsb.f32()[:, 0:1], scale=1.0),
                             reads=[bankq.all(), epsb.all()], writes=[rq.all()])
                        S.op("act", lambda: A.activation(out=rq.f32(), in_=rq.f32(), func=AF.Exp, scale=-0.5), reads=[rq.all()], writes=[rq.all()])
                        PS.put(bankq)
                        for q in sqs:
                            AR.free(q)
                        AR.free(hT)
                        ckpt(31)
                        QT = AR.alloc("QT", 8 * n * 2); QTv = v3(QT.bf(), 8)
                        Cr = AR.alloc("Cr", n * 4); Sr = AR.alloc("Sr", n * 4)
                        if tg.sample:
                            S.dma("sp", Cr.f32(), c4s[:, :], writes=[Cr.all()]); S.dma("sp", Sr.f32(), s4s[:, :], writes=[Sr.all()])
                        else:
                            S.dma("sp", Cr.f32(), c4p[:, tg.t0:tg.t0 + n], writes=[Cr.all()]); S.dma("sp", Sr.f32(), s4p[:, tg.t0:tg.t0 + n], writes=[Sr.all()])
                        S.op("pool", lambda: G.tensor_tensor(out=Cr.f32(), in0=Cr.f32(), in1=rq.f32(), op=ALU.mult), reads=[Cr.all(), rq.all()], writes=[Cr.all()])
                        S.op("pool", lambda: G.tensor_tensor(out=Sr.f32(), in0=Sr.f32(), in1=rq.f32(), op=ALU.mult), reads=[Sr.all(), rq.all()], writes=[Sr.all()])
                        ckpt(32)
                        for gq_ in range(2):
                            ba = PS.get(); bb = PS.get()
                            for j in range(6):
                                mm(ba.t[:, 0:n], wqa[:, j, gq_ * 128:(gq_ + 1) * 128], cqv[:, j, :], j == 0, j == 5, [wqa_b.all(), cqT.all()], [ba.all()])
                            for j in range(6):
                                mm(bb.t[:, 0:n], wqb[:, j, gq_ * 128:(gq_ + 1) * 128], cqv[:, j, :], j == 0, j == 5, [wqb_b.all(), cqT.all()], [bb.all()])
                            ta = AR.alloc("ta", n * 4); tb = AR.alloc("tb", n * 4)
                            S.op("dve", lambda ba=ba, ta=ta: V.tensor_tensor(out=ta.f32(), in0=ba.t[:, 0:n], in1=Cr.f32(), op=ALU.mult), reads=[ba.all(), Cr.all()], writes=[ta.all()])
                            S.op("dve", lambda bb=bb, tb=tb: V.tensor_tensor(out=tb.f32(), in0=bb.t[:, 0:n], in1=Sr.f32(), op=ALU.mult), reads=[bb.all(), Sr.all()], writes=[tb.all()])
                            PS.put(ba); PS.put(bb)
                            for hh in range(4):
                                h = gq_ * 4 + hh
                                S.op("dve", lambda hh=hh, h=h, ta=ta, tb=tb: V.tensor_tensor(out=QTv[64:96, h, :], in0=ta.f32(32 * hh, 32 * hh + 32), in1=tb.f32(32 * hh, 32 * hh + 32), op=ALU.add),
                                     reads=[ta.all(), tb.all()], writes=[QT.all()])
                            AR.free(ta); AR.free(tb)
                        AR.free(Cr); AR.free(Sr)
                        for pr in range(4):
                            bk = PS.get()
                            for j in range(6):
                                mm(bk.t[:, 0:n], wqn[:, j, pr * 128:(pr + 1) * 128], cqv[:, j, :], j == 0, j == 5, [wqn_b.all(), cqT.all()], [bk.all()])
                            S.op("dve", lambda pr=pr, bk=bk: V.tensor_tensor(out=QTv[0:64, 2 * pr, :], in0=bk.t[0:64, 0:n], in1=rq.f32(0, 64), op=ALU.mult),
                                 reads=[bk.all(), rq.all()], writes=[QT.all()])
                            S.op("dve", lambda pr=pr, bk=bk: V.tensor_tensor(out=QTv[0:64, 2 * pr + 1, :], in0=bk.t[64:128, 0:n], in1=rq.f32(0, 64), op=ALU.mult),
                                 reads=[bk.all(), rq.all()], writes=[QT.all()])
                            PS.put(bk)
                        AR.free(rq); AR.free(cqT)
                        ckpt(3)

                        attn = AR.alloc("attn", 4 * n * 4); attnv = v3(attn.f32(), 4)
                        if not tg.sample:
                            Gi = tg.g
                            nkt = 4 * Gi + 4
                            pTs = [AR.alloc("pT%d" % i, 512 * 2) for i in range(4)]
                            rden = AR.alloc("rden", 512 * 4)
                            LOOK = 2
                            tiles = [(h, j) for h in range(8) for j in range(nkt)]
                            bos = {}
                            pend = []

                            def emit_S(idx):
                                h, j = tiles[idx]
                                col0 = max(0, 128 * j - 512 * Gi)
                                bs = PS.get()
                                mm(bs.t[:, col0:512], KTv[0:96, h, 128 * j:128 * j + 128], QTv[0:96, h, col0:512], True, True,
                                   [KT.b(j // 4), QT.all()], [bs.all()])
                                pT = pTs[idx % len(pTs)]
                                S.op("act", lambda: A.activation(out=pT.bf()[:, col0:512], in_=bs.t[:, col0:512], func=AF.Exp, scale=SCALE),
                                     reads=[bs.all()], writes=[pT.all()])
                                PS.put(bs)
                                if j >= 4 * Gi:
                                    S.op("pool", lambda: G.memset(pT.bf(64, 128)[:, col0:col0 + 64], 0.0), writes=[pT.all()])
                                return (h, j, col0, pT)

                            def emit_PV(item):
                                h, j, col0, pT = item
                                pr = h // 2; hf = h % 2
                                if j == 0:
                                    bos[h] = (PS.get(), PS.get())
                                bnum, bden = bos[h]
                                mm(bnum.t[0:64, col0:512], Vtv[:, j, h * 64:(h + 1) * 64], pT.bf()[:, col0:512], j == 0, j == nkt - 1, [Vt.b(j // 4), pT.all()], [bnum.all()])
                                mm(bden.t[0:64, col0:512], ones.bf()[:, 0:64], pT.bf()[:, col0:512], j == 0, j == nkt - 1, [ones.all(), pT.all()], [bden.all()])
                                if j == nkt - 1:
                                    S.op("dve", lambda: V.reciprocal(out=rden.f32(0, 64), in_=bden.t[0:64, :]), reads=[bden.all()], writes=[rden.all()])
                                    PS.put(bden)
                                    S.op("dve", lambda: V.tensor_tensor(out=attnv[64 * hf:64 * hf + 64, pr, :], in0=bnum.t[0:64, :], in1=rden.f32(0, 64), op=ALU.mult),
                                         reads=[bnum.all(), rden.all()], writes=[attn.all()])
                                    PS.put(bnum)
                                    del bos[h]

                            for idx in range(len(tiles)):
                                pend.append(emit_S(idx))
                                if len(pend) > LOOK:
                                    emit_PV(pend.pop(0))
                            while pend:
                                emit_PV(pend.pop(0))
                            for b in pTs:
                                AR.free(b)
                            AR.free(rden)
                        else:
                            wuk_b = load_w("wuk", 2 * 512 * 2, [((lambda b, k=k: b.bf().rearrange("p (k h d) -> p k h d", k=2, h=8)[:, k, :, :]), w_ukv_v[:, k, :, 0, :]) for k in range(2)])
                            wuv_b = load_w("wuv", 2 * 512 * 2, [((lambda b, k=k: b.bf().rearrange("p (k h d) -> p k h d", k=2, h=8)[:, k, :, :]), w_ukv_v[:, k, :, 1, :]) for k in range(2)])
                            wuk = v3(wuk_b.bf(), 2); wuv = v3(wuv_b.bf(), 2)
                            bnum = PS.get(); bden = PS.get()
                            pTs = [AR.alloc("pTs%d" % i, 256 * 2) for i in range(3)]
                            cnt = 0
                            first = [True]

                            def attend_S(KTv_, KTdep, Vv_, Vdep, kp):
                                nonlocal cnt
                                bs = PS.get()
                                for h in range(8):
                                    mm(bs.t[0:kp, h * NS:(h + 1) * NS], KTv_[0:96, h, :], QTv[0:96, h, :], True, True, KTdep + [QT.all()], [bs.all()])
                                pT = pTs[cnt % 3]; cnt += 1
                                S.op("act", lambda: A.activation(out=pT.bf(0, kp), in_=bs.t[0:kp, 0:256], func=AF.Exp, scale=SCALE), reads=[bs.all()], writes=[pT.all()])
                                PS.put(bs)
                                return (pT, Vv_, Vdep, kp)

                            def attend_PV(item):
                                pT, Vv_, Vdep, kp = item
                                for h in range(8):
                                    mm(bnum.t[0:64, h * NS:(h + 1) * NS], Vv_[:, h * 64:(h + 1) * 64], pT.bf(0, kp)[:, h * NS:(h + 1) * NS], first[0], False,
                                       Vdep + [pT.all()], [bnum.all()])
                                    mm(bden.t[0:64, h * NS:(h + 1) * NS], ones.bf(0, kp)[:, 0:64], pT.bf(0, kp)[:, h * NS:(h + 1) * NS], first[0], False,
                                       [ones.all(), pT.all()], [bden.all()])
                                    first[0] = False

                            pend_s = []
                            pend_s.append(attend_S(KTsv, [KTs.all()], Vts.bf(0, NS), [Vts.all()], NS))
                            for ch in range(PAST // 512):
                                cp = AR.alloc("cp", 4 * 256 * 2); cpv = v3(cp.bf(), 4)
                                kp_ = AR.alloc("kp", 4 * 32 * 2); kpv = v3(kp_.bf(), 4)
                                S.dma("pool", cpv, ckvp[l, ch * 512:(ch + 1) * 512, :].rearrange("(a p) f -> p a f", p=128), writes=[cp.all()])
                                S.dma("pool", kpv, krp[l, ch * 512:(ch + 1) * 512, :].rearrange("(a p) f -> p a f", p=128), writes=[kp_.all()])
                                KTc = AR.alloc("KTc", 8 * 512 * 2); KTcv = v3(KTc.bf(), 8)
                                Vc = AR.alloc("Vc", 4 * 512 * 2); Vcv = v3(Vc.bf(), 4)
                                kv_from_tokmajor([(cpv[:, i, :], [cp.all()]) for i in range(4)], [(kpv[:, i, :], [kp_.all()]) for i in range(4)], 4, 128, l, wuk, wuv,
                                                 KTcv, [KTc.all()], lambda i, Vcv=Vcv: Vcv[:, i, :], [Vc.all()])
                                for i in range(4):
                                    pend_s.append(attend_S(KTcv[:, :, i * 128:(i + 1) * 128], [KTc.all()], Vcv[:, i, :], [Vc.all()], 128))
                                    if len(pend_s) > 1:
                                        attend_PV(pend_s.pop(0))
                                while pend_s:
                                    attend_PV(pend_s.pop(0))
                                for b in (cp, kp_, KTc, Vc):
                                    AR.free(b)
                            rden = AR.alloc("rdens", 256 * 4)
                            S.op("dve", lambda: V.reciprocal(out=rden.f32(0, 64)[:, 0:256], in_=bden.t[0:64, 0:256]), reads=[bden.all()], writes=[rden.all()])
                            PS.put(bden)
                            for h in range(8):
                                pr = h // 2; hf = h % 2
                                cs_ = slice(h * NS, (h + 1) * NS)
                                S.op("dve", lambda hf=hf, pr=pr, cs_=cs_: V.tensor_tensor(out=attnv[64 * hf:64 * hf + 64, pr, :], in0=bnum.t[0:64, cs_], in1=rden.f32(0, 64)[:, cs_], op=ALU.mult),
                                     reads=[bnum.all(), rden.all()], writes=[attn.all()])
                            PS.put(bnum)
                            AR.free(rden)
                            for b in pTs:
                                AR.free(b)
                            AR.free(wuk_b); AR.free(wuv_b)
                        AR.free(QT)
                        ckpt(4)
                        ra = rstd_from_chunks([(attnv[:, j, :], [attn.all()]) for j in range(4)], 512, n, "at")
                        atn = AR.alloc("atn", 4 * n * 2); atnv = v3(atn.bf(), 4)
                        for j in range(4):
                            S.op("dve", lambda j=j: V.scalar_tensor_tensor(out=atnv[:, j, :], in0=attnv[:, j, :], scalar=gao.f32()[:, l * 4 + j:l * 4 + j + 1],
                                                                           in1=ra.f32(), op0=ALU.mult, op1=ALU.mult),
                                 reads=[attn.all(), gao.all(), ra.all()], writes=[atn.all()])
                        AR.free(ra); AR.free(attn)
                        ckpt(50)
                        mixs = [AR.alloc("mix%d" % i, n * 4) for i in range(8)]
                        bankm = PS.get()
                        sqs = [AR.alloc("m_sq%d" % i, n * 2) for i in range(2)]
                        for oc in range(8):
                            bk = PS.get()
                            for j in range(4):
                                mm(bk.t[:, 0:n], wov[:, j, oc * 128:(oc + 1) * 128], tg.cvn[:, j, :], j == 0, False, [wo_b.all(), tg.cvnd], [bk.all()])
                            for j in range(4):
                                mm(bk.t[:, 0:n], wov[:, 4 + j, oc * 128:(oc + 1) * 128], atnv[:, j, :], False, j == 3, [wo_b.all(), atn.all()], [bk.all()])
                            if oc > 0:
                                mm(bankm.t[:, 0:n], ones.bf(), prev_sq.bf(), oc == 1, False, [ones.all(), prev_sq.all()], [bankm.all()])
                            S.op("dve", lambda oc=oc, bk=bk: V.tensor_copy(out=mixs[oc].f32(), in_=bk.t[:, 0:n]), reads=[bk.all()], writes=[mixs[oc].all()])
                            PS.put(bk)
                            sq = sqs[oc % 2]
                            S.op("act", lambda oc=oc, sq=sq: A.activation(out=sq.bf(), in_=mixs[oc].f32(), func=AF.Square, scale=1.0 / 32.0), reads=[mixs[oc].all()], writes=[sq.all()])
                            prev_sq = sq
                            if oc == 7:
                                mm(bankm.t[:, 0:n], ones.bf(), prev_sq.bf(), False, True, [ones.all(), prev_sq.all()], [bankm.all()])
                        rm = AR.alloc("rm", n * 4)
                        S.op("act", lambda: A.activation(out=rm.f32(), in_=bankm.t[:, 0:n], func=AF.Ln, bias=epsb.f32()[:, 0:1], scale=1.0),
                             reads=[bankm.all(), epsb.all()], writes=[rm.all()])
                        S.op("act", lambda: A.activation(out=rm.f32(), in_=rm.f32(), func=AF.Exp, scale=-0.5), reads=[rm.all()], writes=[rm.all()])
                        PS.put(bankm)
                        ckpt(51)
                        for q in sqs:
                            AR.free(q)
                        for oc in range(8):
                            S.op("dve", lambda oc=oc: V.scalar_tensor_tensor(out=mixs[oc].f32(), in0=mixs[oc].f32(), scalar=gpo.f32()[:, l * 8 + oc:l * 8 + oc + 1],
                                                                             in1=rm.f32(), op0=ALU.mult, op1=ALU.mult),
                                 reads=[mixs[oc].all(), gpo.all(), rm.all()], writes=[mixs[oc].all()])
                            if oc % 2 == 0:
                                S.op("pool", lambda oc=oc: G.tensor_tensor(out=tg.x[:, oc, :], in0=tg.x[:, oc, :], in1=mixs[oc].f32(), op=ALU.add),
                                     reads=[tg.xd, mixs[oc].all()], writes=[tg.xd])
                            else:
                                S.op("dve", lambda oc=oc: V.tensor_tensor(out=tg.x[:, oc, :], in0=tg.x[:, oc, :], in1=mixs[oc].f32(), op=ALU.add),
                                     reads=[tg.xd, mixs[oc].all()], writes=[tg.xd])
                        AR.free(rm); AR.free(atn)
                        for b_ in mixs:
                            AR.free(b_)
                        ckpt(5)
                    for b in (wcq_b, wqn_b, wqa_b, wqb_b, wo_b, KT, Vt, cvn, cvns, KTs, Vts):
                        AR.free(b)

                    wgv = w_gate[l].rearrange("(k p) m -> p k m", p=128)
                    wuv_ = w_up[l].rearrange("(k p) m -> p k m", p=128)
                    blocks = [ptgs[i:i + 2] for i in range(0, NTG, 2)]
                    if s == 0:
                        blocks[-1] = blocks[-1] + [tgs[-1]]
                    def ffn_F0(blk):
                        ntok = sum(t.n for t in blk)
                        hTb = AR.alloc("hTb", 8 * ntok * 2, 8 * len(blk))
                        hbv = v3(hTb.bf(), 8)
                        offs = []
                        o_ = 0
                        for ti_, tg in enumerate(blk):
                            offs.append(o_)
                            make_hT(tg, gfp, l, "f", dst=(hTb, hbv[:, :, o_:o_ + tg.n], ti_ * 8))
                            o_ += tg.n
                        return (hTb, hbv, offs, ntok)

                    nxt = ffn_F0(blocks[0])
                    for bi, blk in enumerate(blocks):
                        hTb, hbv, offs, ntok = nxt
                        act = AR.alloc("act", NFC * ntok * 2, NFC); actv = v3(act.bf(), NFC)
                        wd_b = AR.alloc("wd", NFC * 1024 * 2, 2)
                        wdv = v3(wd_b.bf(), NFC)
                        wps = [AR.alloc("wgu%d" % i, 2 * 8 * 128 * 2) for i in range(3)]
                        sgs = [AR.alloc("sg%d" % i, 512 * 4) for i in range(3)]
                        sgi = 0
                        for c in range(NFC):
                            wp = wps[c % 3]
                            wpv = wp.bf().rearrange("p (e k m) -> p e k m", e=2, k=8)
                            S.dma("pool", wpv[:, 0, :, :], wgv[:, :, c * 128:(c + 1) * 128], writes=[wp.all()])
                            S.dma("pool", wpv[:, 1, :, :], wuv_[:, :, c * 128:(c + 1) * 128], writes=[wp.all()])
                            if c == 2:
                                wdsrc = w_down[l].rearrange("(c p) m -> p c m", p=128)
                                S.dma("pool", wdv[:, 0:11, :], wdsrc[:, 0:11, :], writes=[wd_b.b(0)])
                            if c == 5:
                                S.dma("pool", wdv[:, 11:22, :], wdsrc[:, 11:22, :], writes=[wd_b.b(1)])
                            for ti, tg in enumerate(blk):
                                n = tg.n; o_ = offs[ti]
                                bg = PS.get(); bu = PS.get()
                                for k in range(8):
                                    mm(bg.t[:, 0:n], wpv[:, 0, k, :], hbv[:, k, o_:o_ + n], k == 0, k == 7, [wp.all(), hTb.b(ti * 8 + k)], [bg.all()])
                                for k in range(8):
                                    mm(bu.t[:, 0:n], wpv[:, 1, k, :], hbv[:, k, o_:o_ + n], k == 0, k == 7, [wp.all(), hTb.b(ti * 8 + k)], [bu.all()])
                                sg = sgs[sgi % 3]; sgi += 1
                                S.op("act", lambda bg=bg, sg=sg, n=n: A.activation(out=sg.f32()[:, 0:n], in_=bg.t[:, 0:n], func=AF.Silu), reads=[bg.all()], writes=[sg.all()])
                                PS.put(bg)
                                S.op("dve", lambda bu=bu, sg=sg, c=c, o_=o_, n=n: V.tensor_tensor(out=actv[:, c, o_:o_ + n], in0=bu.t[:, 0:n], in1=sg.f32()[:, 0:n], op=ALU.mult),
                                     reads=[bu.all(), sg.all()], writes=[act.b(c)])
                                PS.put(bu)
                        for wp in wps:
                            AR.free(wp)
                        for b_ in sgs:
                            AR.free(b_)
                        AR.free(hTb)
                        if bi + 1 < len(blocks):
                            nxt = ffn_F0(blocks[bi + 1])
                        else:
                            wpg_b = load_w("wpg", 8 * 1024 * 2, [(lambda b: v3(b.bf(), 8), w_pgate[l].rearrange("(k p) m -> p k m", p=128))])
                            wpp_b = load_w("wpp", 2 * 1024 * 2, [(lambda b: v3(b.bf(), 2), w_pproj[l].rearrange("(k p) m -> p k m", p=128))])
                        for ti, tg in enumerate(blk):
                            n = tg.n; o_ = offs[ti]
                            f = AR.alloc("f", 8 * n * 4); fv = v3(f.f32(), 8)
                            bankf = PS.get()
                            sqs = [AR.alloc("f_sq%d" % i, n * 2) for i in range(2)]
                            prev_sq = None
                            for oc in range(8):
                                bk = PS.get()
                                for c in range(NFC):
                                    mm(bk.t[:, 0:n], wdv[:, c, oc * 128:(oc + 1) * 128], actv[:, c, o_:o_ + n], c == 0, c == NFC - 1, [wd_b.b(c // 11), act.b(c)], [bk.all()])
                                if prev_sq is not None:
                                    mm(bankf.t[:, 0:n], ones.bf(), prev_sq.bf(), oc == 1, False, [ones.all(), prev_sq.all()], [bankf.all()])
                                S.op("dve", lambda oc=oc, bk=bk: V.tensor_copy(out=fv[:, oc, :], in_=bk.t[:, 0:n]), reads=[bk.all()], writes=[f.all()])
                                PS.put(bk)
                                sq = sqs[oc % 2]
                                S.op("act", lambda oc=oc, sq=sq: A.activation(out=sq.bf(), in_=fv[:, oc, :], func=AF.Square, scale=1.0 / 32.0), reads=[f.all()], writes=[sq.all()])
                                prev_sq = sq
                            mm(bankf.t[:, 0:n], ones.bf(), prev_sq.bf(), False, True, [ones.all(), prev_sq.all()], [bankf.all()])
                            rf = AR.alloc("rf", n * 4)
                            S.op("act", lambda: A.activation(out=rf.f32(), in_=bankf.t[:, 0:n], func=AF.Ln, bias=epsb.f32()[:, 0:1], scale=1.0),
                                 reads=[bankf.all(), epsb.all()], writes=[rf.all()])
                            S.op("act", lambda: A.activation(out=rf.f32(), in_=rf.f32(), func=AF.Exp, scale=-0.5), reads=[rf.all()], writes=[rf.all()])
                            PS.put(bankf)
                            for q in sqs:
                                AR.free(q)
                            for oc in range(8):
                                S.op("dve", lambda oc=oc: V.scalar_tensor_tensor(out=fv[:, oc, :], in0=fv[:, oc, :], scalar=gfo.f32()[:, l * 8 + oc:l * 8 + oc + 1],
                                                                                 in1=rf.f32(), op0=ALU.mult, op1=ALU.mult),
                                     reads=[f.all(), gfo.all(), rf.all()], writes=[f.all()])
                                S.op("dve", lambda oc=oc, tg=tg: V.tensor_tensor(out=tg.x[:, oc, :], in0=tg.x[:, oc, :], in1=fv[:, oc, :], op=ALU.add),
                                     reads=[tg.xd, f.all()], writes=[tg.xd])
                            AR.free(rf); AR.free(f)
                        AR.free(act); AR.free(wd_b)
                        ckpt(6)

                    wpgv = v3(wpg_b.bf(), 8); wppv = v3(wpp_b.bf(), 2)
                    if l == 0:
                        pending_A[0] = load_A_weights(1)
                    elif s + 1 < NSEQ:
                        pending_A[0] = load_A_weights(0)
                    ptks = []
                    for tg in tgs:
                        ptk = AR.alloc("ptk", tg.ntile * 256 * 2); ptkv = v3(ptk.bf(), tg.ntile)
                        if tg.sample:
                            S.dma("pool", ptkv[0:tg.tp, 0, :], psm[l][:, :], writes=[ptk.all()])
                        else:
                            S.dma("pool", ptkv, pp[l, tg.s, tg.t0:tg.t0 + tg.n, :].rearrange("(a p) f -> p a f", p=128), writes=[ptk.all()])
                        ptks.append((ptk, ptkv))
                    for tgi, tg in enumerate(tgs):
                        n = tg.n; tp = tg.tp
                        xb = AR.alloc("xb", 8 * n * 2); xbv = v3(xb.bf(), 8)
                        S.op("dve", lambda tg=tg: V.tensor_copy(out=xbv[:, 0:4, :], in_=tg.x[:, 0:4, :]), reads=[tg.xd], writes=[xb.all()])
                        S.op("act", lambda tg=tg: A.activation(out=xbv[:, 4:8, :], in_=tg.x[:, 4:8, :], func=AF.Copy), reads=[tg.xd], writes=[xb.all()])
                        ptk, ptkv = ptks[tgi]
                        bT = PS.get(); bTb = bT.t[:, :].bitcast(BF16)
                        for i in range(tg.ntile):
                            for c in range(2):
                                S.op("pe", lambda i=i, c=c: PE.transpose(bTb[:, c * 512 + i * tp:c * 512 + (i + 1) * tp], ptkv[0:tp, i, c * 128:(c + 1) * 128],
                                                                         identb.bf()[0:tp, 0:tp]),
                                     reads=[ptk.all(), identb.all()], writes=[bT.all()])
                        pT_ = AR.alloc("pT_", 2 * n * 2); pTv = v3(pT_.bf(), 2)
                        for c in range(2):
                            S.op("act", lambda c=c: A.activation(out=pTv[:, c, :], in_=bTb[:, c * 512:c * 512 + n], func=AF.Copy), reads=[bT.all()], writes=[pT_.all()])
                        PS.put(bT); AR.free(ptk)
                        sgps = [AR.alloc("sgp%d" % i, n * 4) for i in range(2)]
                        for oc in range(8):
                            bg = PS.get(); bp = PS.get()
                            for k in range(8):
                                mm(bg.t[:, 0:n], wpgv[:, k, oc * 128:(oc + 1) * 128], xbv[:, k, :], k == 0, k == 7, [wpg_b.all(), xb.all()], [bg.all()])
                            for c in range(2):
                                mm(bp.t[:, 0:n], wppv[:, c, oc * 128:(oc + 1) * 128], pTv[:, c, :], c == 0, c == 1, [wpp_b.all(), pT_.all()], [bp.all()])
                            sg = sgps[oc % 2]
                            S.op("act", lambda bg=bg, sg=sg: A.activation(out=sg.f32(), in_=bg.t[:, 0:n], func=AF.Sigmoid), reads=[bg.all()], writes=[sg.all()])
                            PS.put(bg)
                            S.op("dve", lambda bp=bp, sg=sg: V.tensor_tensor(out=sg.f32(), in0=bp.t[:, 0:n], in1=sg.f32(), op=ALU.mult), reads=[bp.all(), sg.all()], writes=[sg.all()])
                            PS.put(bp)
                            S.op("dve", lambda oc=oc, tg=tg, sg=sg: V.tensor_tensor(out=tg.x[:, oc, :], in0=tg.x[:, oc, :], in1=sg.f32(), op=ALU.add),
                                 reads=[tg.xd, sg.all()], writes=[tg.xd])
                        AR.free(xb); AR.free(pT_)
                        for b_ in sgps:
                            AR.free(b_)
                    AR.free(wpg_b); AR.free(wpp_b)
                    ckpt(7)

                youts = [AR.alloc("yout%d" % i, 1024 * 4) for i in range(2)]

                def store_y(srcv, dep, tp, dst_ap, idx):
                    yb = youts[idx % 2]
                    for half in range(2):
                        bk = PS.get()
                        for cc in range(4):
                            c = half * 4 + cc
                            S.op("pe", lambda c=c, cc=cc, bk=bk: PE.transpose(bk.t[0:tp, cc * 128:(cc + 1) * 128], srcv[:, c, :], ident.f32()),
                                 reads=dep + [ident.all()], writes=[bk.all()])
                        if half == 0:
                            S.op("dve", lambda bk=bk, yb=yb: V.tensor_copy(out=yb.f32(0, tp)[:, 0:512], in_=bk.t[0:tp, :]), reads=[bk.all()], writes=[yb.all()])
                        else:
                            S.op("act", lambda bk=bk, yb=yb: A.activation(out=yb.f32(0, tp)[:, 512:1024], in_=bk.t[0:tp, :], func=AF.Copy), reads=[bk.all()], writes=[yb.all()])
                        PS.put(bk)
                    S.dma("sp", dst_ap, yb.f32(0, tp), reads=[yb.all()])

                for tt in range(NTT):
                    store_y(xTv[:, :, tt * 128:(tt + 1) * 128], [xT.b(tt // 4)], 128, y_p[s, tt * 128:(tt + 1) * 128, :], tt)
                if s == 0:
                    store_y(xsTv, [xsT.all()], NS, y_s[:, :], 0)
                for b in youts:
                    AR.free(b)


        except _Stop:
            pass
        S.finish()
        stats = dict(ninstr=dict(S.ninstr), nwait=dict(S.nwait), peak_bytes=AR.peak * 4)
    return nc, stats


def rope_tables(pos):
    half = 16
    inv = (10000.0 ** (-np.arange(half, dtype=np.float32) / np.float32(half))).astype(np.float32)
    ang = pos.astype(np.float32)[:, None] * inv[None, :]
    cos = np.cos(ang).astype(np.float32); sin = np.sin(ang).astype(np.float32)
    c32 = np.concatenate([cos.T, cos.T], axis=0)
    s32 = np.concatenate([-sin.T, sin.T], axis=0)
    c4 = np.ascontiguousarray(np.tile(c32, (4, 1))); s4 = np.ascontiguousarray(np.tile(s32, (4, 1)))
    return c4, s4, np.ascontiguousarray(cos), np.ascontiguousarray(sin)


_PROG_CACHE = {}


def run(inputs, NSEQ, T, PAST, ncores=8):
    key = (NSEQ, T, PAST)
    if key not in _PROG_CACHE:
        _PROG_CACHE[key] = build_program(NSEQ, T, PAST)
    nc, stats = _PROG_CACHE[key]
    f = lambda a: np.ascontiguousarray(np.asarray(a, dtype=np.float32))
    c4p, s4p, ctp, stp = rope_tables(np.arange(T))
    c4s, s4s, cts, sts = rope_tables(PAST + np.arange(32))
    wnames = ["g_mix_pre", "w_in", "w_conv", "g_q", "w_uq", "g_kv", "w_ukv", "g_conv_out", "g_attn_out", "w_o", "g_mix_post",
              "g_ffn_pre", "w_ffn_gate", "w_ffn_up", "w_ffn_down", "g_ffn_post", "w_ple_proj", "w_ple_gate"]
    shared = {k: f(inputs[k]) for k in wnames}
    shared.update(identd=np.eye(128, dtype=np.float32), c4p=c4p, s4p=s4p, c4s=c4s, s4s=s4s, ctp=ctp, stp=stp, cts=cts, sts=sts)
    xp = f(inputs["x_prompt"]); xs = f(inputs["x_sample"]); ck = f(inputs["cache_kv_latent"]); kr = f(inputs["cache_k_rope"])
    sc = f(inputs["state_conv"]); pp = f(inputs["p_prompt"]); ps = f(inputs["p_sample"])
    in_maps = []
    for c in range(ncores):
        m = dict(shared)
        m["xp"] = np.ascontiguousarray(xp[c * NSEQ:(c + 1) * NSEQ]); m["xs"] = np.ascontiguousarray(xs[c])
        m["ckvp"] = np.ascontiguousarray(ck[:, c]); m["krp"] = np.ascontiguousarray(kr[:, c]); m["cst"] = np.ascontiguousarray(sc[:, c])
        m["pp"] = np.ascontiguousarray(pp[:, c * NSEQ:(c + 1) * NSEQ]); m["psm"] = np.ascontiguousarray(ps[:, c])
        in_maps.append(m)
    res = run_bass_kernel_spmd(nc, in_maps, core_ids=list(range(ncores)))
    R = res.results
    y_p = np.concatenate([r["y_p"] for r in R], axis=0)
    y_s = np.stack([r["y_s"] for r in R], axis=0)
    lat_p = np.concatenate([r["lat_p"] for r in R], axis=1)
    kr_p = np.concatenate([r["kr_p"] for r in R], axis=1)
    conv_p = np.concatenate([r["conv_p"] for r in R], axis=1)
    lat_s = np.stack([r["lat_s"] for r in R], axis=1)
    kr_s = np.stack([r["kr_s"] for r in R], axis=1)
    conv_s = np.stack([r["conv_s"] for r in R], axis=1)
    return tuple(np.ascontiguousarray(a, dtype=np.float32) for a in (y_p, y_s, lat_p, kr_p, conv_p, lat_s, kr_s, conv_s))


def kernel(**inputs):
    return run(inputs, 4, 2048, 4096, 8)
```

```python
import numpy as np
import contextlib
import concourse.bass as bass
import concourse.mybir as mybir
from concourse.bass_utils import run_bass_kernel_spmd

F32 = mybir.dt.float32
BF16 = mybir.dt.bfloat16
ALU = mybir.AluOpType
AF = mybir.ActivationFunctionType
AX = mybir.AxisListType

SEM_LIMIT = 30000
N_DMA_SEMS = 24


class Tile:
    def __init__(self, name, nblocks=1):
        self.name = name
        self.nb = nblocks
        self.w = [None] * nblocks
        self.r = [dict() for _ in range(nblocks)]

    def all(self):
        return (self, 0, self.nb)

    def b(self, i, n=1):
        assert 0 <= i and i + n <= self.nb, (self.name, i, n, self.nb)
        return (self, i, i + n)

    def events(self):
        evs = []
        for b in range(self.nb):
            if self.w[b] is not None:
                evs.append(self.w[b])
            evs.extend(self.r[b].values())
        return evs


class Sched:
    def __init__(self, nc, es, same_engine_sync=True):
        self.nc = nc
        self.es = es
        self.eng = {"pe": nc.tensor, "act": nc.scalar, "dve": nc.vector,
                    "pool": nc.gpsimd, "sp": nc.sync}
        self.same_sync = same_engine_sync
        self.nkeys = 0
        self.cur = {}
        self.seen = {e: {} for e in self.eng}
        self.ninstr = {e: 0 for e in self.eng}
        self.nwait = {e: 0 for e in self.eng}
        for e in self.eng:
            self._new_sem(e)
        self.dma_sems = {"hw": [], "sw": []}
        for kind in ("hw", "sw"):
            for i in range(N_DMA_SEMS // 2):
                h = es.enter_context(nc.semaphore("dsem_%s%d" % (kind, i)))
                self.dma_sems[kind].append((self._key(), h, 0))
        self.dma_rr = {"hw": 0, "sw": 0}

    def _key(self):
        self.nkeys += 1
        return self.nkeys

    def _new_sem(self, e):
        h = self.es.enter_context(self.nc.semaphore("s_%s_%d" % (e, self.nkeys)))
        self.cur[e] = [self._key(), h, 0]

    def _collect(self, e, reads, writes, extra=()):
        need = {}
        own = self.cur[e][0]
        seen = self.seen[e]

        def req(ev):
            if ev is None:
                return
            k, h, v = ev
            if k == own and (e == "pe" or not self.same_sync):
                return
            if seen.get(k, 0) >= v:
                return
            if k not in need or need[k][2] < v:
                need[k] = ev

        for (t, lo, hi) in reads:
            for b in range(lo, hi):
                req(t.w[b])
                if getattr(t, "excl", False):
                    for ev in t.r[b].values():
                        if ev[0] != own:
                            req(ev)
        for (t, lo, hi) in writes:
            for b in range(lo, hi):
                req(t.w[b])
                for ev in t.r[b].values():
                    req(ev)
        for ev in extra:
            req(ev)
        for k, (kk, h, v) in need.items():
            self.eng[e].wait_ge(h, v)
            seen[k] = v
            self.nwait[e] += 1

    def _update(self, ev, reads, writes):
        for (t, lo, hi) in writes:
            for b in range(lo, hi):
                t.w[b] = ev
                t.r[b] = {}
        for (t, lo, hi) in reads:
            for b in range(lo, hi):
                if t.w[b] is ev:
                    continue
                t.r[b][ev[0]] = ev

    def op(self, e, fn, reads=(), writes=()):
        self._collect(e, reads, writes)
        ins = fn()
        c = self.cur[e]
        c[2] += 1
        ev = (c[0], c[1], c[2])
        ins.then_inc(c[1], 1)
        self.ninstr[e] += 1
        self._update(ev, reads, writes)
        if c[2] >= SEM_LIMIT:
            self._new_sem(e)
        return ev

    def dma(self, q, out, in_, reads=(), writes=(), **kw):
        kind = "sw" if q == "pool" else "hw"
        pool_ = self.dma_sems[kind]
        slot = self.dma_rr[kind]
        self.dma_rr[kind] = (slot + 1) % len(pool_)
        k, h, v = pool_[slot]
        extra = [(k, h, v)] if v > 0 else []
        self._collect(q, reads, writes, extra)
        ins = self.eng[q].dma_start(out=out, in_=in_, **kw)
        ins.then_inc(h, 16)
        ev = (k, h, v + 16)
        pool_[slot] = ev
        self.ninstr[q] += 1
        self._update(ev, reads, writes)
        return ev

    def finish(self):
        for (k, h, v) in self.dma_sems["hw"] + self.dma_sems["sw"]:
            if v > 0 and self.seen["sp"].get(k, 0) < v:
                self.nc.sync.wait_ge(h, v)
                self.seen["sp"][k] = v


class Buf(Tile):
    def __init__(self, arena, name, off, nwords, nblocks):
        super().__init__(name, nblocks)
        self.arena = arena
        self.off = off
        self.nwords = nwords

    def f32(self, p0=0, p1=128):
        return self.arena.ap[p0:p1, self.off:self.off + self.nwords]

    def bf(self, p0=0, p1=128):
        return self.arena.ap[p0:p1, self.off:self.off + self.nwords].bitcast(BF16)


class Arena:
    def __init__(self, ap, nwords):
        self.ap = ap
        self.nwords = nwords
        self.free_list = [(0, nwords)]
        self.ghosts = []
        self.peak = 0
        self.live = {}

    def alloc(self, name, nbytes, nblocks=1):
        nw = (nbytes + 3) // 4
        nw = (nw + 7) // 8 * 8
        for i, (o, n) in enumerate(self.free_list):
            if n >= nw:
                if n == nw:
                    self.free_list.pop(i)
                else:
                    self.free_list[i] = (o + nw, n - nw)
                b = Buf(self, name, o, nw, nblocks)
                self.peak = max(self.peak, o + nw)
                inh = {}
                keep = []
                for (go, ge, evs) in self.ghosts:
                    if go < o + nw and ge > o:
                        for ev in evs:
                            if ev[0] not in inh or inh[ev[0]][2] < ev[2]:
                                inh[ev[0]] = ev
                        if go >= o and ge <= o + nw:
                            continue
                    keep.append((go, ge, evs))
                self.ghosts = keep
                for blk in range(nblocks):
                    b.r[blk] = dict(inh)
                self.live[id(b)] = b
                return b
        raise RuntimeError("arena OOM allocating %s (%d bytes); free=%s" % (name, nbytes, self.free_list))

    def free(self, b):
        del self.live[id(b)]
        evs = {}
        for ev in b.events():
            if ev[0] not in evs or evs[ev[0]][2] < ev[2]:
                evs[ev[0]] = ev
        self.ghosts.append((b.off, b.off + b.nwords, list(evs.values())))
        fl = self.free_list + [(b.off, b.nwords)]
        fl.sort()
        out = []
        for (o, n) in fl:
            if out and out[-1][0] + out[-1][1] == o:
                out[-1] = (out[-1][0], out[-1][1] + n)
            else:
                out.append((o, n))
        self.free_list = out


class PsumPool:
    def __init__(self, nc, es, nbanks=8):
        self.banks = []
        for i in range(nbanks):
            t = es.enter_context(nc.psum_tensor("psb%d" % i, [128, 512], F32))
            tl = Tile("psb%d" % i, 1)
            tl.excl = True
            tl.t = t
            self.banks.append(tl)
        self.free = list(self.banks)

    def get(self):
        assert self.free, "PSUM pool exhausted"
        return self.free.pop(0)

    def put(self, b):
        self.free.append(b)

EPS = 1e-6
SCALE = 96 ** -0.5
DFF = 2816
NFC = 22


_NO_SAMPLE = False


class _Stop(Exception):
    pass


def build_program(NSEQ, T, PAST, stop_at=None):
    nc = bass.Bass("TRN2", target_bir_lowering=False)
    NTG = T // 512
    NTT = T // 128
    NS = 32

    def din(name, shape):
        return nc.dram_tensor(name, list(shape), F32, kind="ExternalInput").ap()

    def dout(name, shape):
        return nc.dram_tensor(name, list(shape), F32, kind="ExternalOutput").ap()

    xp = din("xp", [NSEQ, T, 1024]); xs = din("xs", [NS, 1024])
    ckvp = din("ckvp", [2, PAST, 256]); krp = din("krp", [2, PAST, 32]); cst = din("cst", [2, 2, 512])
    pp = din("pp", [2, NSEQ, T, 256]); psm = din("psm", [2, NS, 256])
    g_mix_pre = din("g_mix_pre", [2, 1024]); w_in = din("w_in", [2, 1024, 2592]); w_conv = din("w_conv", [2, 3, 512])
    g_q = din("g_q", [2, 768]); w_uq = din("w_uq", [2, 768, 768]); g_kv = din("g_kv", [2, 256])
    w_ukv = din("w_ukv", [2, 256, 1024]); g_conv_out = din("g_conv_out", [2, 512]); g_attn_out = din("g_attn_out", [2, 512])
    w_o = din("w_o", [2, 1024, 1024]); g_mix_post = din("g_mix_post", [2, 1024]); g_ffn_pre = din("g_ffn_pre", [2, 1024])
    w_gate = din("w_ffn_gate", [2, 1024, DFF]); w_up = din("w_ffn_up", [2, 1024, DFF]); w_down = din("w_ffn_down", [2, DFF, 1024])
    g_ffn_post = din("g_ffn_post", [2, 1024]); w_pproj = din("w_ple_proj", [2, 256, 1024]); w_pgate = din("w_ple_gate", [2, 1024, 1024])
    identd = din("identd", [128, 128])
    c4p = din("c4p", [128, T]); s4p = din("s4p", [128, T]); c4s = din("c4s", [128, NS]); s4s = din("s4s", [128, NS])
    ctp = din("ctp", [T, 16]); stp = din("stp", [T, 16]); cts = din("cts", [NS, 16]); sts = din("sts", [NS, 16])

    y_p = dout("y_p", [NSEQ, T, 1024]); y_s = dout("y_s", [NS, 1024])
    lat_p = dout("lat_p", [2, NSEQ, T, 256]); kr_p = dout("kr_p", [2, NSEQ, T, 32]); conv_p = dout("conv_p", [2, NSEQ, 2, 512])
    lat_s = dout("lat_s", [2, NS, 256]); kr_s = dout("kr_s", [2, NS, 32]); conv_s = dout("conv_s", [2, 2, 512])

    es = contextlib.ExitStack()
    with es, nc.allow_low_precision("bf16 matmul operands, fp32 accumulation"), \
            nc.allow_non_contiguous_dma("small strided parameter / state transfers"):
        S = Sched(nc, es)
        NW = 212800 // 4
        ar_t = es.enter_context(nc.sbuf_tensor("arena", [128, NW], F32))
        AR = Arena(ar_t, NW)
        PS = PsumPool(nc, es)
        V = nc.vector; A = nc.scalar; G = nc.gpsimd; PE = nc.tensor

        def mm(out, lhsT, rhs, start, stop, reads, writes):
            S.op("pe", lambda: PE.matmul(out, lhsT=lhsT, rhs=rhs, start=start, stop=stop, skip_group_check=True),
                 reads, writes)

        def v3(ap, a):
            return ap.rearrange("p (a b) -> p a b", a=a)

        ident = AR.alloc("ident", 128 * 4); identb = AR.alloc("identb", 128 * 2); ones = AR.alloc("ones", 128 * 2)
        epsb = AR.alloc("epsb", 4)
        S.dma("sp", ident.f32(), identd[:, :], writes=[ident.all()])
        S.dma("pool", identb.bf(), identd[:, :], writes=[identb.all()])
        S.op("dve", lambda: V.memset(ones.bf(), 1.0), writes=[ones.all()])
        S.op("dve", lambda: V.memset(epsb.f32(), EPS), writes=[epsb.all()])
        gmp = AR.alloc("gmp", 16 * 4); gq = AR.alloc("gq", 12 * 4); gco = AR.alloc("gco", 8 * 4); gao = AR.alloc("gao", 8 * 4)
        gpo = AR.alloc("gpo", 16 * 4); gfp = AR.alloc("gfp", 16 * 4); gfo = AR.alloc("gfo", 16 * 4); wcv = AR.alloc("wcv", 24 * 4)
        gkvb = AR.alloc("gkvb", 512 * 4)
        for (buf, src, nchunk) in ((gmp, g_mix_pre, 8), (gq, g_q, 6), (gco, g_conv_out, 4), (gao, g_attn_out, 4),
                                   (gpo, g_mix_post, 8), (gfp, g_ffn_pre, 8), (gfo, g_ffn_post, 8)):
            for l in range(2):
                S.dma("sp", buf.f32()[:, l * nchunk:(l + 1) * nchunk], src[l].rearrange("(c p) -> p c", p=128),
                      writes=[buf.all()])
        for l in range(2):
            for j in range(3):
                o = (l * 3 + j) * 4
                S.dma("sp", wcv.f32()[:, o:o + 4], w_conv[l, j].rearrange("(c p) -> p c", p=128), writes=[wcv.all()])
            S.dma("sp", gkvb.f32()[:, l * 256:(l + 1) * 256], g_kv[l].partition_broadcast(128), writes=[gkvb.all()])
        ctk = AR.alloc("ctk", NTT * 16 * 4); stk = AR.alloc("stk", NTT * 16 * 4)
        S.dma("sp", v3(ctk.f32(), NTT), ctp.rearrange("(a p) f -> p a f", p=128), writes=[ctk.all()])
        S.dma("sp", v3(stk.f32(), NTT), stp.rearrange("(a p) f -> p a f", p=128), writes=[stk.all()])
        ctks = AR.alloc("ctks", 16 * 4); stks = AR.alloc("stks", 16 * 4)
        S.dma("sp", ctks.f32(0, NS), cts[:, :], writes=[ctks.all()])
        S.dma("sp", stks.f32(0, NS), sts[:, :], writes=[stks.all()])

        xT = AR.alloc("xT", 8 * T * 4, NTG); xTv = v3(xT.f32(), 8)
        xsT = AR.alloc("xsT", 8 * NS * 4, 1); xsTv = v3(xsT.f32(), 8)
        uh = AR.alloc("uh", 8 * 4, 1); uhv = v3(uh.f32(), 4)
        class TG:
            pass

        def make_tgs(s, cvnv, cvn, cvnsv, cvns):
            tgs = []
            for g in range(NTG):
                t = TG(); t.sample = False; t.g = g; t.n = 512; t.t0 = g * 512; t.s = s
                t.x = xTv[:, :, t.t0:t.t0 + 512]; t.xd = xT.b(g)
                t.cvn = cvnv[:, :, t.t0:t.t0 + 512]; t.cvnd = cvn.b(g)
                t.ntile = 4; t.tp = 128
                tgs.append(t)
            if s == 0 and not _NO_SAMPLE:
                t = TG(); t.sample = True; t.g = NTG; t.n = NS; t.t0 = 0; t.s = 0
                t.x = xsTv; t.xd = xsT.all(); t.cvn = cvnsv; t.cvnd = cvns.all(); t.ntile = 1; t.tp = NS
                tgs.append(t)
            return tgs

        def rstd_from_chunks(chunks, D, n, name):
            bank = PS.get()
            sqs = [AR.alloc(name + "_sq%d" % i, n * 2) for i in range(2)]
            sc = float(D) ** -0.5
            for i, (ap, rd) in enumerate(chunks):
                sq = sqs[i % 2]
                S.op("act", lambda ap=ap, sq=sq: A.activation(out=sq.bf(), in_=ap, func=AF.Square, scale=sc),
                     reads=rd, writes=[sq.all()])
                mm(bank.t[:, 0:n], ones.bf(), sq.bf(), i == 0, i == len(chunks) - 1, [ones.all(), sq.all()], [bank.all()])
            rb = AR.alloc(name + "_rb", n * 4)
            S.op("act", lambda: A.activation(out=rb.f32(), in_=bank.t[:, 0:n], func=AF.Ln, bias=epsb.f32()[:, 0:1], scale=1.0),
                 reads=[bank.all(), epsb.all()], writes=[rb.all()])
            S.op("act", lambda: A.activation(out=rb.f32(), in_=rb.f32(), func=AF.Exp, scale=-0.5), reads=[rb.all()], writes=[rb.all()])
            PS.put(bank)
            for q in sqs:
                AR.free(q)
            return rb

        def make_hT(tg, gbuf, l, name, dst=None):
            n = tg.n
            if dst is None:
                hT = AR.alloc(name + "_hT", 8 * n * 2, 8)
                hv = v3(hT.bf(), 8)
                boff = 0
            else:
                hT, hv, boff = dst
            rb = rstd_from_chunks([(tg.x[:, c, :], [tg.xd]) for c in range(8)], 1024, n, name)
            for c in range(8):
                S.op("dve", lambda c=c: V.scalar_tensor_tensor(out=hv[:, c, :], in0=tg.x[:, c, :], scalar=gbuf.f32()[:, l * 8 + c:l * 8 + c + 1],
                                                               in1=rb.f32(), op0=ALU.mult, op1=ALU.mult),
                     reads=[tg.xd, gbuf.all(), rb.all()], writes=[hT.b(boff + c)])
            AR.free(rb)
            return hT, hv

        def load_w(name, nbytes, dmas):
            b = AR.alloc(name, nbytes)
            for (ov, iv) in dmas:
                S.dma("pool", ov(b), iv, writes=[b.all()])
            return b

        def kv_from_tokmajor(ckv_bf_tiles, kr_bf_tiles, ntile, tp, l, wuk, wuv, KT_dst, KT_dep, V_dst_fn, V_dep):
            n = ntile * tp
            bT = PS.get(); bR = PS.get()
            bTb = bT.t[:, :].bitcast(BF16)
            bRb = bR.t[:, :].bitcast(BF16)
            for i in range(ntile):
                (cap, cdep) = ckv_bf_tiles[i]
                (kap, kdep) = kr_bf_tiles[i]
                for c in range(2):
                    S.op("pe", lambda c=c, i=i, cap=cap: PE.transpose(bTb[:, c * 512 + i * tp:c * 512 + (i + 1) * tp], cap[:, c * 128:(c + 1) * 128],
                                                                     identb.bf()[0:tp, 0:tp]),
                         reads=cdep + [identb.all()], writes=[bT.all()])
                S.op("pe", lambda i=i, kap=kap: PE.transpose(bRb[0:32, i * tp:(i + 1) * tp], kap, identb.bf()[0:tp, 0:tp]),
                     reads=kdep + [identb.all()], writes=[bR.all()])
            ckvT = AR.alloc("ckvT", 2 * n * 2)
            cTv = v3(ckvT.bf(), 2)
            for c in range(2):
                S.op("act", lambda c=c: A.activation(out=cTv[:, c, :], in_=bTb[:, c * 512:c * 512 + n], func=AF.Copy),
                     reads=[bT.all()], writes=[ckvT.all()])
            krs = AR.alloc("krs", n * 2)
            S.op("dve", lambda: V.tensor_copy(out=krs.bf(0, 32)[:, 0:n], in_=bRb[0:32, 0:n]), reads=[bR.all()], writes=[krs.all()])
            PS.put(bT); PS.put(bR)
            for h in range(8):
                if h % 2 == 0:
                    S.op("dve", lambda h=h: V.tensor_copy(out=KT_dst[64:96, h, :], in_=krs.bf(0, 32)[:, 0:n]), reads=[krs.all()], writes=KT_dep)
                else:
                    S.op("act", lambda h=h: A.activation(out=KT_dst[64:96, h, :], in_=krs.bf(0, 32)[:, 0:n], func=AF.Copy), reads=[krs.all()], writes=KT_dep)
            for pr in range(4):
                bk = PS.get()
                for c in range(2):
                    mm(bk.t[:, 0:n], wuk[:, c, pr * 128:(pr + 1) * 128], cTv[:, c, :], c == 0, c == 1, [wuk_b.all(), ckvT.all()], [bk.all()])
                S.op("act", lambda pr=pr, bk=bk: A.activation(out=KT_dst[0:64, 2 * pr, :], in_=bk.t[0:64, 0:n], func=AF.Copy), reads=[bk.all()], writes=KT_dep)
                S.op("dve", lambda pr=pr, bk=bk: V.tensor_copy(out=KT_dst[0:64, 2 * pr + 1, :], in_=bk.t[64:128, 0:n]), reads=[bk.all()], writes=KT_dep)
                PS.put(bk)
            for i in range(ntile):
                bv = PS.get()
                for c in range(2):
                    mm(bv.t[0:tp, :], cTv[:, c, i * tp:(i + 1) * tp], wuv[:, c, :], c == 0, c == 1, [wuv_b.all(), ckvT.all()], [bv.all()])
                vd = V_dst_fn(i)
                if i % 2 == 0:
                    S.op("dve", lambda vd=vd, bv=bv: V.tensor_copy(out=vd, in_=bv.t[0:tp, :]), reads=[bv.all()], writes=V_dep)
                else:
                    S.op("act", lambda vd=vd, bv=bv: A.activation(out=vd, in_=bv.t[0:tp, :], func=AF.Copy), reads=[bv.all()], writes=V_dep)
                PS.put(bv)
            AR.free(ckvT); AR.free(krs)

        wuk_b = None; wuv_b = None

        def ckpt(i):
            if stop_at is not None and i == stop_at:
                raise _Stop()

        pending_A = [None]

        def load_A_weights(l_):
            w_in_v_ = w_in[l_].rearrange("(k p) m -> p k m", p=128)
            w_ukv_v_ = w_ukv[l_].rearrange("(k p) (h e d) -> p k h e d", p=128, h=8, e=2)
            wkv_b_ = load_w("wkv", 8 * 288 * 2, [(lambda b: v3(b.bf(), 8), w_in_v_[:, :, 2304:2592])])
            wuk_b_ = load_w("wuk", 2 * 512 * 2, [((lambda b, k=k: b.bf().rearrange("p (k h d) -> p k h d", k=2, h=8)[:, k, :, :]), w_ukv_v_[:, k, :, 0, :]) for k in range(2)])
            wuv_b_ = load_w("wuv", 2 * 512 * 2, [((lambda b, k=k: b.bf().rearrange("p (k h d) -> p k h d", k=2, h=8)[:, k, :, :]), w_ukv_v_[:, k, :, 1, :]) for k in range(2)])
            wcv_b_ = AR.alloc("wconv", 8 * 1536 * 2, 12)
            wv_ = v3(wcv_b_.bf(), 8)
            for j in range(4):
                for kind in (2, 1, 0):
                    c0 = kind * 512 + j * 128
                    S.dma("pool", wv_[:, :, c0:c0 + 128], w_in_v_[:, :, c0:c0 + 128], writes=[wcv_b_.b(kind * 4 + j)])
            return (wcv_b_, wkv_b_, wuk_b_, wuv_b_)

        try:
            for s in range(NSEQ):
                xin = [AR.alloc("xin%d" % i, 1024 * 4) for i in range(2)]

                def load_x(src_ap, tp, dstv, dep, idx):
                    xb = xin[idx % 2]
                    S.dma("sp", xb.f32(0, tp), src_ap, writes=[xb.all()])
                    for half in range(2):
                        bk = PS.get()
                        for cc in range(4):
                            c = half * 4 + cc
                            S.op("pe", lambda c=c, cc=cc, bk=bk, xb=xb: PE.transpose(bk.t[:, cc * 128:cc * 128 + tp], xb.f32(0, tp)[:, c * 128:(c + 1) * 128],
                                                                                ident.f32()[0:tp, 0:tp]),
                                 reads=[xb.all(), ident.all()], writes=[bk.all()])
                        src = bk.t[:, :].rearrange("p (a b) -> p a b", a=4)[:, :, 0:tp]
                        if half == 0:
                            S.op("dve", lambda src=src: V.tensor_copy(out=dstv[:, 0:4, :], in_=src), reads=[bk.all()], writes=dep)
                        else:
                            S.op("act", lambda src=src: A.activation(out=dstv[:, 4:8, :], in_=src, func=AF.Copy), reads=[bk.all()], writes=dep)
                        PS.put(bk)

                for tt in range(NTT):
                    load_x(xp[s, tt * 128:(tt + 1) * 128, :], 128, xTv[:, :, tt * 128:(tt + 1) * 128], [xT.b(tt // 4)], tt)
                if s == 0:
                    load_x(xs[:, :], NS, xsTv, [xsT.all()], 0)
                for b in xin:
                    AR.free(b)
                ckpt(0)

                for l in range(2):
                    KT = AR.alloc("KT", 8 * T * 2, NTG); KTv = v3(KT.bf(), 8)
                    Vt = AR.alloc("Vt", NTT * 512 * 2, NTG); Vtv = v3(Vt.bf(), NTT)
                    cvn = AR.alloc("cvn", 4 * T * 2, NTG); cvnv = v3(cvn.bf(), 4)
                    cvns = AR.alloc("cvns", 4 * NS * 2, 1); cvnsv = v3(cvns.bf(), 4)
                    KTs = AR.alloc("KTs", 8 * NS * 2, 1); KTsv = v3(KTs.bf(), 8)
                    Vts = AR.alloc("Vts", 512 * 2, 1)
                    tgs = make_tgs(s, cvnv, cvn, cvnsv, cvns)
                    ptgs = tgs[:NTG]
                    w_in_v = w_in[l].rearrange("(k p) m -> p k m", p=128)
                    if pending_A[0] is not None:
                        wcv_b, wkv_b, wuk_b, wuv_b = pending_A[0]
                        pending_A[0] = None
                    else:
                        wcv_b, wkv_b, wuk_b, wuv_b = load_A_weights(l)
                    wcvv = v3(wcv_b.bf(), 8)
                    wkvv = v3(wkv_b.bf(), 8)
                    w_ukv_v = w_ukv[l].rearrange("(k p) (h e d) -> p k h e d", p=128, h=8, e=2)
                    wuk = v3(wuk_b.bf(), 2); wuv = v3(wuv_b.bf(), 2)

                    for tgi, tg in enumerate(tgs):
                        n = tg.n
                        hT, hv = make_hT(tg, gmp, l, "a")
                        tp = tg.tp; nt = tg.ntile
                        cbs = []; kbs = []; tmp_bufs = []
                        krr = AR.alloc("krr", nt * 32 * 4); kro = AR.alloc("kro", nt * 32 * 4); kt = AR.alloc("kt", nt * 32 * 4); kbf = AR.alloc("kbf", nt * 32 * 2)
                        krrv = v3(krr.f32(0, tp), nt); krov = v3(kro.f32(0, tp), nt); ktv = v3(kt.f32(0, tp), nt); kbfv = v3(kbf.bf(0, tp), nt)
                        ssb = AR.alloc("ssb", 8 * 4); junk = AR.alloc("junk", 256 * 2)
                        S.op("pool", lambda: G.memset(ssb.f32(), 0.0), writes=[ssb.all()])
                        bkvs = []
                        for i in range(nt):
                            bkv = PS.get(); bkvs.append(bkv)
                            for k in range(8):
                                mm(bkv.t[0:tp, 0:288], hv[:, k, i * tp:(i + 1) * tp], wkvv[:, k, :], k == 0, k == 7, [hT.b(k), wkv_b.all()], [bkv.all()])
                            S.op("act", lambda bkv=bkv, i=i: A.activation(out=junk.bf(0, tp), in_=bkv.t[0:tp, 0:256], func=AF.Square, scale=1.0 / 16.0,
                                                                          accum_out=ssb.f32(0, tp)[:, i:i + 1]),
                                 reads=[bkv.all()], writes=[junk.all(), ssb.all()])
                            S.op("act", lambda bkv=bkv, i=i: A.activation(out=krrv[:, i, :], in_=bkv.t[0:tp, 256:288], func=AF.Copy), reads=[bkv.all()], writes=[krr.all()])
                        S.op("act", lambda: A.activation(out=ssb.f32(0, tp)[:, 0:nt], in_=ssb.f32(0, tp)[:, 0:nt], func=AF.Ln, bias=epsb.f32(0, tp)[:, 0:1], scale=1.0),
                             reads=[ssb.all(), epsb.all()], writes=[ssb.all()])
                        S.op("act", lambda: A.activation(out=ssb.f32(0, tp)[:, 0:nt], in_=ssb.f32(0, tp)[:, 0:nt], func=AF.Exp, scale=-0.5), reads=[ssb.all()], writes=[ssb.all()])
                        for i in range(nt):
                            bkv = bkvs[i]
                            ckv = AR.alloc("ckv", 256 * 4); cbf = AR.alloc("cbf", 256 * 2)
                            S.op("dve", lambda bkv=bkv, ckv=ckv, i=i: V.scalar_tensor_tensor(out=ckv.f32(0, tp), in0=bkv.t[0:tp, 0:256], scalar=ssb.f32(0, tp)[:, i:i + 1],
                                                                                             in1=gkvb.f32(0, tp)[:, l * 256:(l + 1) * 256], op0=ALU.mult, op1=ALU.mult),
                                 reads=[bkv.all(), ssb.all(), gkvb.all()], writes=[ckv.all()])
                            PS.put(bkv)
                            S.op("act", lambda ckv=ckv, cbf=cbf: A.activation(out=cbf.bf(0, tp), in_=ckv.f32(0, tp), func=AF.Copy), reads=[ckv.all()], writes=[cbf.all()])
                            if tg.sample:
                                S.dma("sp", lat_s[l][:, :], ckv.f32(0, tp), reads=[ckv.all()])
                            else:
                                r0 = tg.t0 + i * 128
                                S.dma("sp", lat_p[l, tg.s, r0:r0 + 128, :], ckv.f32(), reads=[ckv.all()])
                            cbs.append((cbf.bf(0, tp), [cbf.all()])); kbs.append((kbfv[:, i, :], [kbf.all()]))
                            tmp_bufs += [ckv, cbf]
                        if tg.sample:
                            cs = ctks.f32(0, tp).rearrange("p (a f) -> p a f", a=1); sn = stks.f32(0, tp).rearrange("p (a f) -> p a f", a=1)
                            cdep = [ctks.all(), stks.all()]
                        else:
                            cs = v3(ctk.f32(), NTT)[:, tg.g * 4:tg.g * 4 + 4, :]; sn = v3(stk.f32(), NTT)[:, tg.g * 4:tg.g * 4 + 4, :]
                            cdep = [ctk.all(), stk.all()]
                        x1 = krrv[:, :, 0:16]; x2 = krrv[:, :, 16:32]
                        o1 = krov[:, :, 0:16]; o2 = krov[:, :, 16:32]
                        t1_ = ktv[:, :, 0:16]; t2_ = ktv[:, :, 16:32]
                        S.op("pool", lambda: G.tensor_tensor(out=o1, in0=x1, in1=cs, op=ALU.mult), reads=[krr.all()] + cdep, writes=[kro.all()])
                        S.op("pool", lambda: G.tensor_tensor(out=t1_, in0=x2, in1=sn, op=ALU.mult), reads=[krr.all()] + cdep, writes=[kt.all()])
                        S.op("pool", lambda: G.tensor_tensor(out=o1, in0=o1, in1=t1_, op=ALU.subtract), reads=[kro.all(), kt.all()], writes=[kro.all()])
                        S.op("pool", lambda: G.tensor_tensor(out=o2, in0=x1, in1=sn, op=ALU.mult), reads=[krr.all()] + cdep, writes=[kro.all()])
                        S.op("pool", lambda: G.tensor_tensor(out=t2_, in0=x2, in1=cs, op=ALU.mult), reads=[krr.all()] + cdep, writes=[kt.all()])
                        S.op("pool", lambda: G.tensor_tensor(out=o2, in0=o2, in1=t2_, op=ALU.add), reads=[kro.all(), kt.all()], writes=[kro.all()])
                        S.op("pool", lambda: G.tensor_copy(out=kbfv, in_=krov), reads=[kro.all()], writes=[kbf.all()])
                        if tg.sample:
                            S.dma("sp", kr_s[l][:, :], kro.f32(0, tp), reads=[kro.all()])
                        else:
                            S.dma("sp", kr_p[l, tg.s, tg.t0:tg.t0 + 512, :].rearrange("(a p) f -> p a f", p=128), krov, reads=[kro.all()])
                        tmp_bufs += [junk, ssb, krr, kro, kt, kbf]
                        u = AR.alloc("u", 4 * (n + 2) * 4); uv = v3(u.f32(), 4)
                        co = AR.alloc("co", 4 * n * 4); cov = v3(co.f32(), 4)
                        if tg.sample:
                            for t_ in range(2):
                                S.dma("sp", uv[:, :, t_], cst[l, t_].rearrange("(c p) -> p c", p=128), writes=[u.all()])
                        elif tg.g == 0:
                            S.op("pool", lambda: G.memset(uv[:, :, 0:2], 0.0), writes=[u.all()])
                        else:
                            S.op("pool", lambda: G.tensor_copy(out=uv[:, :, 0:2], in_=uhv), reads=[uh.all()], writes=[u.all()])
                        for j in range(4):
                            bx = PS.get(); bc = PS.get(); bb = PS.get()
                            for (bank, off) in ((bx, 1024), (bc, 512), (bb, 0)):
                                for k in range(8):
                                    mm(bank.t[:, 0:n], wcvv[:, k, off + j * 128:off + (j + 1) * 128], hv[:, k, :], k == 0, k == 7,
                                       [wcv_b.b((off // 512) * 4 + j), hT.b(k)], [bank.all()])
                            xsb = AR.alloc("xsb", n * 4)
                            S.op("act", lambda bx=bx, xsb=xsb: A.activation(out=xsb.f32(), in_=bx.t[:, 0:n], func=AF.Copy), reads=[bx.all()], writes=[xsb.all()])
                            PS.put(bx)
                            S.op("dve", lambda j=j, bc=bc, xsb=xsb: V.tensor_tensor(out=uv[:, j, 2:2 + n], in0=bc.t[:, 0:n], in1=xsb.f32(), op=ALU.mult),
                                 reads=[bc.all(), xsb.all()], writes=[u.all()])
                            PS.put(bc); AR.free(xsb)
                            t1 = AR.alloc("t1", n * 4)
                            wo_ = l * 12
                            S.op("act", lambda j=j, t1=t1: A.activation(out=t1.f32(), in_=uv[:, j, 0:n], func=AF.Copy, scale=wcv.f32()[:, wo_ + j:wo_ + j + 1]),
                                 reads=[u.all(), wcv.all()], writes=[t1.all()])
                            S.op("dve", lambda j=j, t1=t1: V.scalar_tensor_tensor(out=t1.f32(), in0=uv[:, j, 1:1 + n], scalar=wcv.f32()[:, wo_ + 4 + j:wo_ + 4 + j + 1],
                                                                                in1=t1.f32(), op0=ALU.mult, op1=ALU.add),
                                 reads=[u.all(), wcv.all(), t1.all()], writes=[t1.all()])
                            S.op("dve", lambda j=j, t1=t1: V.scalar_tensor_tensor(out=t1.f32(), in0=uv[:, j, 2:2 + n], scalar=wcv.f32()[:, wo_ + 8 + j:wo_ + 8 + j + 1],
                                                                                in1=t1.f32(), op0=ALU.mult, op1=ALU.add),
                                 reads=[u.all(), wcv.all(), t1.all()], writes=[t1.all()])
                            S.op("dve", lambda j=j, bb=bb, t1=t1: V.tensor_tensor(out=cov[:, j, :], in0=bb.t[:, 0:n], in1=t1.f32(), op=ALU.mult),
                                 reads=[bb.all(), t1.all()], writes=[co.all()])
                            PS.put(bb); AR.free(t1)
                        if not tg.sample:
                            if tg.g < NTG - 1:
                                S.op("pool", lambda: G.tensor_copy(out=uhv, in_=uv[:, :, n:n + 2]), reads=[u.all()], writes=[uh.all()])
                            else:
                                for t_ in range(2):
                                    S.dma("sp", conv_p[l, tg.s, t_].rearrange("(c p) -> p c", p=128), uv[:, :, n + t_], reads=[u.all()])
                        else:
                            for t_ in range(2):
                                S.dma("sp", conv_s[l, t_].rearrange("(c p) -> p c", p=128), uv[:, :, n + t_], reads=[u.all()])
                        AR.free(hT)
                        if tg.sample:
                            kv_from_tokmajor(cbs, kbs, 1, NS, l, wuk, wuv, KTsv, [KTs.all()], lambda i: Vts.bf(0, NS), [Vts.all()])
                        else:
                            g = tg.g
                            kv_from_tokmajor(cbs, kbs, 4, 128, l, wuk, wuv, KTv[:, :, tg.t0:tg.t0 + 512], [KT.b(g)],
                                             lambda i, g=g: Vtv[:, g * 4 + i, :], [Vt.b(g)])
                        rc = rstd_from_chunks([(cov[:, j, :], [co.all()]) for j in range(4)], 512, n, "c")
                        for j in range(4):
                            S.op("dve", lambda j=j: V.scalar_tensor_tensor(out=tg.cvn[:, j, :], in0=cov[:, j, :], scalar=gco.f32()[:, l * 4 + j:l * 4 + j + 1],
                                                                           in1=rc.f32(), op0=ALU.mult, op1=ALU.mult),
                                 reads=[co.all(), gco.all(), rc.all()], writes=[tg.cvnd])
                        AR.free(rc); AR.free(u); AR.free(co)
                        ckpt(1)
                        for b in tmp_bufs:
                            AR.free(b)
                        ckpt(2)
                    AR.free(wcv_b); AR.free(wkv_b)
                    AR.free(wuk_b); AR.free(wuv_b)

                    wcq_b = AR.alloc("wcq", 8 * 768 * 2, 6)
                    wcqv = v3(wcq_b.bf(), 8)
                    for j in range(6):
                        S.dma("pool", wcqv[:, :, j * 128:(j + 1) * 128], w_in_v[:, :, 1536 + j * 128:1536 + (j + 1) * 128], writes=[wcq_b.b(j)])
                    w_uq_v = w_uq[l].rearrange("(k p) (h d) -> p k h d", p=128, h=8)
                    wuq_b = load_w("wuq", 6 * 768 * 2, [(lambda b: v3(b.bf(), 6), w_uq[l].rearrange("(k p) m -> p k m", p=128))])
                    wuq4 = wuq_b.bf().rearrange("p (k h d) -> p k h d", k=6, h=8)

                    def q4(b):
                        return b.bf().rearrange("p (k h d) -> p k h d", k=6, h=8)
                    wqn_b = AR.alloc("wqn", 6 * 512 * 2); wqa_b = AR.alloc("wqa", 6 * 256 * 2); wqb_b = AR.alloc("wqb", 6 * 256 * 2)
                    for k in range(6):
                        S.op("pool", lambda k=k: G.tensor_copy(out=q4(wqn_b)[:, k, :, :], in_=wuq4[:, k, :, 0:64]), reads=[wuq_b.all()], writes=[wqn_b.all()])
                        S.op("pool", lambda k=k: G.tensor_copy(out=q4(wqa_b)[:, k, :, :], in_=wuq4[:, k, :, 64:96]), reads=[wuq_b.all()], writes=[wqa_b.all()])
                        S.op("pool", lambda k=k: G.tensor_copy(out=q4(wqb_b)[:, k, :, 0:16], in_=wuq4[:, k, :, 80:96]), reads=[wuq_b.all()], writes=[wqb_b.all()])
                        S.op("pool", lambda k=k: G.tensor_copy(out=q4(wqb_b)[:, k, :, 16:32], in_=wuq4[:, k, :, 64:80]), reads=[wuq_b.all()], writes=[wqb_b.all()])
                    AR.free(wuq_b)
                    wqn = v3(wqn_b.bf(), 6); wqa = v3(wqa_b.bf(), 6); wqb = v3(wqb_b.bf(), 6)
                    wo_b = load_w("wo", 8 * 1024 * 2, [(lambda b: v3(b.bf(), 8), w_o[l].rearrange("(k p) m -> p k m", p=128))])
                    wov = v3(wo_b.bf(), 8)
                    ckpt(30)

                    for tg in tgs:
                        n = tg.n
                        hT, hv = make_hT(tg, gmp, l, "b")
                        cqT = AR.alloc("cqT", 6 * n * 2); cqv = v3(cqT.bf(), 6)
                        bankq = PS.get()
                        sqs = [AR.alloc("cq_sq%d" % i, n * 2) for i in range(2)]
                        for j in range(6):
                            bk = PS.get()
                            for k in range(8):
                                mm(bk.t[:, 0:n], wcqv[:, k, j * 128:(j + 1) * 128], hv[:, k, :], k == 0, k == 7, [wcq_b.b(j), hT.b(k)], [bk.all()])
                            S.op("act", lambda j=j, bk=bk: A.activation(out=cqv[:, j, :], in_=bk.t[:, 0:n], func=AF.Copy, scale=gq.f32()[:, l * 6 + j:l * 6 + j + 1]),
                                 reads=[bk.all(), gq.all()], writes=[cqT.all()])
                            sq = sqs[j % 2]
                            S.op("act", lambda bk=bk, sq=sq: A.activation(out=sq.bf(), in_=bk.t[:, 0:n], func=AF.Square, scale=768.0 ** -0.5), reads=[bk.all()], writes=[sq.all()])
                            PS.put(bk)
                            if j > 0:
                                mm(bankq.t[:, 0:n], ones.bf(), prev_sq.bf(), j == 1, False, [ones.all(), prev_sq.all()], [bankq.all()])
                            prev_sq = sq
                            if j == 5:
                                mm(bankq.t[:, 0:n], ones.bf(), prev_sq.bf(), False, True, [ones.all(), prev_sq.all()], [bankq.all()])
                        rq = AR.alloc("rq", n * 4)
                        S.op("act", lambda: A.activation(out=rq.f32(), in_=bankq.t[:, 0:n], func=AF.Ln, bias=epsb.f32()[:, 0:1], scale=1.0),
                             reads=[bankq.all(), epsb.all()], writes=[rq.all()])
                        S.op("act", lambda: A.activation(out=rq.f32(), in_=rq.f32(), func=AF.Exp, scale=-0.5), reads=[rq.all()], writes=[rq.all()])
                        PS.put(bankq)
                        for q in sqs:
                            AR.free(q)
                        AR.free(hT)
                        ckpt(31)
                        QT = AR.alloc("QT", 8 * n * 2); QTv = v3(QT.bf(), 8)
                        Cr = AR.alloc("Cr", n * 4); Sr = AR.alloc("Sr", n * 4)
                        if tg.sample:
                            S.dma("sp", Cr.f32(), c4s[:, :], writes=[Cr.all()]); S.dma("sp", Sr.f32(), s4s[:, :], writes=[Sr.all()])
                        else:
                            S.dma("sp", Cr.f32(), c4p[:, tg.t0:tg.t0 + n], writes=[Cr.all()]); S.dma("sp", Sr.f32(), s4p[:, tg.t0:tg.t0 + n], writes=[Sr.all()])
                        S.op("pool", lambda: G.tensor_tensor(out=Cr.f32(), in0=Cr.f32(), in1=rq.f32(), op=ALU.mult), reads=[Cr.all(), rq.all()], writes=[Cr.all()])
                        S.op("pool", lambda: G.tensor_tensor(out=Sr.f32(), in0=Sr.f32(), in1=rq.f32(), op=ALU.mult), reads=[Sr.all(), rq.all()], writes=[Sr.all()])
                        ckpt(32)
                        for gq_ in range(2):
                            ba = PS.get(); bb = PS.get()
                            for j in range(6):
                                mm(ba.t[:, 0:n], wqa[:, j, gq_ * 128:(gq_ + 1) * 128], cqv[:, j, :], j == 0, j == 5, [wqa_b.all(), cqT.all()], [ba.all()])
                            for j in range(6):
                                mm(bb.t[:, 0:n], wqb[:, j, gq_ * 128:(gq_ + 1) * 128], cqv[:, j, :], j == 0, j == 5, [wqb_b.all(), cqT.all()], [bb.all()])
                            ta = AR.alloc("ta", n * 4); tb = AR.alloc("tb", n * 4)
                            S.op("dve", lambda ba=ba, ta=ta: V.tensor_tensor(out=ta.f32(), in0=ba.t[:, 0:n], in1=Cr.f32(), op=ALU.mult), reads=[ba.all(), Cr.all()], writes=[ta.all()])
                            S.op("dve", lambda bb=bb, tb=tb: V.tensor_tensor(out=tb.f32(), in0=bb.t[:, 0:n], in1=Sr.f32(), op=ALU.mult), reads=[bb.all(), Sr.all()], writes=[tb.all()])
                            PS.put(ba); PS.put(bb)
                            for hh in range(4):
                                h = gq_ * 4 + hh
                                S.op("dve", lambda hh=hh, h=h, ta=ta, tb=tb: V.tensor_tensor(out=QTv[64:96, h, :], in0=ta.f32(32 * hh, 32 * hh + 32), in1=tb.f32(32 * hh, 32 * hh + 32), op=ALU.add),
                                     reads=[ta.all(), tb.all()], writes=[QT.all()])
                            AR.free(ta); AR.free(tb)
                        AR.free(Cr); AR.free(Sr)
                        for pr in range(4):
                            bk = PS.get()
                            for j in range(6):
                                mm(bk.t[:, 0:n], wqn[:, j, pr * 128:(pr + 1) * 128], cqv[:, j, :], j == 0, j == 5, [wqn_b.all(), cqT.all()], [bk.all()])
                            S.op("dve", lambda pr=pr, bk=bk: V.tensor_tensor(out=QTv[0:64, 2 * pr, :], in0=bk.t[0:64, 0:n], in1=rq.f32(0, 64), op=ALU.mult),
                                 reads=[bk.all(), rq.all()], writes=[QT.all()])
                            S.op("dve", lambda pr=pr, bk=bk: V.tensor_tensor(out=QTv[0:64, 2 * pr + 1, :], in0=bk.t[64:128, 0:n], in1=rq.f32(0, 64), op=ALU.mult),
                                 reads=[bk.all(), rq.all()], writes=[QT.all()])
                            PS.put(bk)
                        AR.free(rq); AR.free(cqT)
                        ckpt(3)

                        attn = AR.alloc("attn", 4 * n * 4); attnv = v3(attn.f32(), 4)
                        if not tg.sample:
                            Gi = tg.g
                            nkt = 4 * Gi + 4
                            pTs = [AR.alloc("pT%d" % i, 512 * 2) for i in range(4)]
                            rden = AR.alloc("rden", 512 * 4)
                            LOOK = 2
                            tiles = [(h, j) for h in range(8) for j in range(nkt)]
                            bos = {}
                            pend = []

                            def emit_S(idx):
                                h, j = tiles[idx]
                                col0 = max(0, 128 * j - 512 * Gi)
                                bs = PS.get()
                                mm(bs.t[:, col0:512], KTv[0:96, h, 128 * j:128 * j + 128], QTv[0:96, h, col0:512], True, True,
                                   [KT.b(j // 4), QT.all()], [bs.all()])
                                pT = pTs[idx % len(pTs)]
                                S.op("act", lambda: A.activation(out=pT.bf()[:, col0:512], in_=bs.t[:, col0:512], func=AF.Exp, scale=SCALE),
                                     reads=[bs.all()], writes=[pT.all()])
                                PS.put(bs)
                                if j >= 4 * Gi:
                                    S.op("pool", lambda: G.memset(pT.bf(64, 128)[:, col0:col0 + 64], 0.0), writes=[pT.all()])
                                return (h, j, col0, pT)

                            def emit_PV(item):
                                h, j, col0, pT = item
                                pr = h // 2; hf = h % 2
                                if j == 0:
                                    bos[h] = (PS.get(), PS.get())
                                bnum, bden = bos[h]
                                mm(bnum.t[0:64, col0:512], Vtv[:, j, h * 64:(h + 1) * 64], pT.bf()[:, col0:512], j == 0, j == nkt - 1, [Vt.b(j // 4), pT.all()], [bnum.all()])
                                mm(bden.t[0:64, col0:512], ones.bf()[:, 0:64], pT.bf()[:, col0:512], j == 0, j == nkt - 1, [ones.all(), pT.all()], [bden.all()])
                                if j == nkt - 1:
                                    S.op("dve", lambda: V.reciprocal(out=rden.f32(0, 64), in_=bden.t[0:64, :]), reads=[bden.all()], writes=[rden.all()])
                                    PS.put(bden)
                                    S.op("dve", lambda: V.tensor_tensor(out=attnv[64 * hf:64 * hf + 64, pr, :], in0=bnum.t[0:64, :], in1=rden.f32(0, 64), op=ALU.mult),
                                         reads=[bnum.all(), rden.all()], writes=[attn.all()])
                                    PS.put(bnum)
                                    del bos[h]

                            for idx in range(len(tiles)):
                                pend.append(emit_S(idx))
                                if len(pend) > LOOK:
                                    emit_PV(pend.pop(0))
                            while pend:
                                emit_PV(pend.pop(0))
                            for b in pTs:
                                AR.free(b)
                            AR.free(rden)
                        else:
                            wuk_b = load_w("wuk", 2 * 512 * 2, [((lambda b, k=k: b.bf().rearrange("p (k h d) -> p k h d", k=2, h=8)[:, k, :, :]), w_ukv_v[:, k, :, 0, :]) for k in range(2)])
                            wuv_b = load_w("wuv", 2 * 512 * 2, [((lambda b, k=k: b.bf().rearrange("p (k h d) -> p k h d", k=2, h=8)[:, k, :, :]), w_ukv_v[:, k, :, 1, :]) for k in range(2)])
                            wuk = v3(wuk_b.bf(), 2); wuv = v3(wuv_b.bf(), 2)
                            bnum = PS.get(); bden = PS.get()
                            pTs = [AR.alloc("pTs%d" % i, 256 * 2) for i in range(3)]
                            cnt = 0
                            first = [True]

                            def attend_S(KTv_, KTdep, Vv_, Vdep, kp):
                                nonlocal cnt
                                bs = PS.get()
                                for h in range(8):
                                    mm(bs.t[0:kp, h * NS:(h + 1) * NS], KTv_[0:96, h, :], QTv[0:96, h, :], True, True, KTdep + [QT.all()], [bs.all()])
                                pT = pTs[cnt % 3]; cnt += 1
                                S.op("act", lambda: A.activation(out=pT.bf(0, kp), in_=bs.t[0:kp, 0:256], func=AF.Exp, scale=SCALE), reads=[bs.all()], writes=[pT.all()])
                                PS.put(bs)
                                return (pT, Vv_, Vdep, kp)

                            def attend_PV(item):
                                pT, Vv_, Vdep, kp = item
                                for h in range(8):
                                    mm(bnum.t[0:64, h * NS:(h + 1) * NS], Vv_[:, h * 64:(h + 1) * 64], pT.bf(0, kp)[:, h * NS:(h + 1) * NS], first[0], False,
                                       Vdep + [pT.all()], [bnum.all()])
                                    mm(bden.t[0:64, h * NS:(h + 1) * NS], ones.bf(0, kp)[:, 0:64], pT.bf(0, kp)[:, h * NS:(h + 1) * NS], first[0], False,
                                       [ones.all(), pT.all()], [bden.all()])
                                    first[0] = False

                            pend_s = []
                            pend_s.append(attend_S(KTsv, [KTs.all()], Vts.bf(0, NS), [Vts.all()], NS))
                            def prep_chunk(ch):
                                cp = AR.alloc("cp", 4 * 256 * 2); cpv = v3(cp.bf(), 4)
                                kp_ = AR.alloc("kp", 4 * 32 * 2); kpv = v3(kp_.bf(), 4)
                                S.dma("pool", cpv, ckvp[l, ch * 512:(ch + 1) * 512, :].rearrange("(a p) f -> p a f", p=128), writes=[cp.all()])
                                S.dma("pool", kpv, krp[l, ch * 512:(ch + 1) * 512, :].rearrange("(a p) f -> p a f", p=128), writes=[kp_.all()])
                                KTc = AR.alloc("KTc", 8 * 512 * 2); KTcv = v3(KTc.bf(), 8)
                                Vc = AR.alloc("Vc", 4 * 512 * 2); Vcv = v3(Vc.bf(), 4)
                                kv_from_tokmajor([(cpv[:, i, :], [cp.all()]) for i in range(4)], [(kpv[:, i, :], [kp_.all()]) for i in range(4)], 4, 128, l, wuk, wuv,
                                                 KTcv, [KTc.all()], lambda i, Vcv=Vcv: Vcv[:, i, :], [Vc.all()])
                                return (KTc, KTcv, Vc, Vcv, (cp, kp_, KTc, Vc))

                            nch = PAST // 512
                            for ch in range(nch):
                                KTc, KTcv, Vc, Vcv, bufs_c = prep_chunk(ch)
                                for i in range(4):
                                    pend_s.append(attend_S(KTcv[:, :, i * 128:(i + 1) * 128], [KTc.all()], Vcv[:, i, :], [Vc.all()], 128))
                                    if len(pend_s) > 1:
                                        attend_PV(pend_s.pop(0))
                                while pend_s:
                                    attend_PV(pend_s.pop(0))
                                for b in bufs_c:
                                    AR.free(b)
                            rden = AR.alloc("rdens", 256 * 4)
                            S.op("dve", lambda: V.reciprocal(out=rden.f32(0, 64)[:, 0:256], in_=bden.t[0:64, 0:256]), reads=[bden.all()], writes=[rden.all()])
                            PS.put(bden)
                            for h in range(8):
                                pr = h // 2; hf = h % 2
                                cs_ = slice(h * NS, (h + 1) * NS)
                                S.op("dve", lambda hf=hf, pr=pr, cs_=cs_: V.tensor_tensor(out=attnv[64 * hf:64 * hf + 64, pr, :], in0=bnum.t[0:64, cs_], in1=rden.f32(0, 64)[:, cs_], op=ALU.mult),
                                     reads=[bnum.all(), rden.all()], writes=[attn.all()])
                            PS.put(bnum)
                            AR.free(rden)
                            for b in pTs:
                                AR.free(b)
                            AR.free(wuk_b); AR.free(wuv_b)
                        AR.free(QT)
                        ckpt(4)
                        ra = rstd_from_chunks([(attnv[:, j, :], [attn.all()]) for j in range(4)], 512, n, "at")
                        atn = AR.alloc("atn", 4 * n * 2); atnv = v3(atn.bf(), 4)
                        for j in range(4):
                            S.op("dve", lambda j=j: V.scalar_tensor_tensor(out=atnv[:, j, :], in0=attnv[:, j, :], scalar=gao.f32()[:, l * 4 + j:l * 4 + j + 1],
                                                                           in1=ra.f32(), op0=ALU.mult, op1=ALU.mult),
                                 reads=[attn.all(), gao.all(), ra.all()], writes=[atn.all()])
                        AR.free(ra); AR.free(attn)
                        ckpt(50)
                        mixs = [AR.alloc("mix%d" % i, n * 4) for i in range(8)]
                        bankm = PS.get()
                        sqs = [AR.alloc("m_sq%d" % i, n * 2) for i in range(2)]
                        for oc in range(8):
                            bk = PS.get()
                            for j in range(4):
                                mm(bk.t[:, 0:n], wov[:, j, oc * 128:(oc + 1) * 128], tg.cvn[:, j, :], j == 0, False, [wo_b.all(), tg.cvnd], [bk.all()])
                            for j in range(4):
                                mm(bk.t[:, 0:n], wov[:, 4 + j, oc * 128:(oc + 1) * 128], atnv[:, j, :], False, j == 3, [wo_b.all(), atn.all()], [bk.all()])
                            if oc > 0:
                                mm(bankm.t[:, 0:n], ones.bf(), prev_sq.bf(), oc == 1, False, [ones.all(), prev_sq.all()], [bankm.all()])
                            S.op("dve", lambda oc=oc, bk=bk: V.tensor_copy(out=mixs[oc].f32(), in_=bk.t[:, 0:n]), reads=[bk.all()], writes=[mixs[oc].all()])
                            PS.put(bk)
                            sq = sqs[oc % 2]
                            S.op("act", lambda oc=oc, sq=sq: A.activation(out=sq.bf(), in_=mixs[oc].f32(), func=AF.Square, scale=1.0 / 32.0), reads=[mixs[oc].all()], writes=[sq.all()])
                            prev_sq = sq
                            if oc == 7:
                                mm(bankm.t[:, 0:n], ones.bf(), prev_sq.bf(), False, True, [ones.all(), prev_sq.all()], [bankm.all()])
                        rm = AR.alloc("rm", n * 4)
                        S.op("act", lambda: A.activation(out=rm.f32(), in_=bankm.t[:, 0:n], func=AF.Ln, bias=epsb.f32()[:, 0:1], scale=1.0),
                             reads=[bankm.all(), epsb.all()], writes=[rm.all()])
                        S.op("act", lambda: A.activation(out=rm.f32(), in_=rm.f32(), func=AF.Exp, scale=-0.5), reads=[rm.all()], writes=[rm.all()])
                        PS.put(bankm)
                        ckpt(51)
                        for q in sqs:
                            AR.free(q)
                        for oc in range(8):
                            S.op("dve", lambda oc=oc: V.scalar_tensor_tensor(out=mixs[oc].f32(), in0=mixs[oc].f32(), scalar=gpo.f32()[:, l * 8 + oc:l * 8 + oc + 1],
                                                                             in1=rm.f32(), op0=ALU.mult, op1=ALU.mult),
                                 reads=[mixs[oc].all(), gpo.all(), rm.all()], writes=[mixs[oc].all()])
                            if oc % 2 == 0:
                                S.op("pool", lambda oc=oc: G.tensor_tensor(out=tg.x[:, oc, :], in0=tg.x[:, oc, :], in1=mixs[oc].f32(), op=ALU.add),
                                     reads=[tg.xd, mixs[oc].all()], writes=[tg.xd])
                            else:
                                S.op("dve", lambda oc=oc: V.tensor_tensor(out=tg.x[:, oc, :], in0=tg.x[:, oc, :], in1=mixs[oc].f32(), op=ALU.add),
                                     reads=[tg.xd, mixs[oc].all()], writes=[tg.xd])
                        AR.free(rm); AR.free(atn)
                        for b_ in mixs:
                            AR.free(b_)
                        ckpt(5)
                    for b in (wcq_b, wqn_b, wqa_b, wqb_b, wo_b, KT, Vt, cvn, cvns, KTs, Vts):
                        AR.free(b)

                    wgv = w_gate[l].rearrange("(k p) m -> p k m", p=128)
                    wuv_ = w_up[l].rearrange("(k p) m -> p k m", p=128)
                    blocks = [ptgs[i:i + 2] for i in range(0, NTG, 2)]
                    if s == 0:
                        blocks[-1] = blocks[-1] + [tgs[-1]]
                    def ffn_F0(blk):
                        ntok = sum(t.n for t in blk)
                        hTb = AR.alloc("hTb", 8 * ntok * 2, 8 * len(blk))
                        hbv = v3(hTb.bf(), 8)
                        offs = []
                        o_ = 0
                        for ti_, tg in enumerate(blk):
                            offs.append(o_)
                            make_hT(tg, gfp, l, "f", dst=(hTb, hbv[:, :, o_:o_ + tg.n], ti_ * 8))
                            o_ += tg.n
                        return (hTb, hbv, offs, ntok)

                    nxt = ffn_F0(blocks[0])
                    for bi, blk in enumerate(blocks):
                        hTb, hbv, offs, ntok = nxt
                        act = AR.alloc("act", NFC * ntok * 2, NFC); actv = v3(act.bf(), NFC)
                        wd_b = AR.alloc("wd", NFC * 1024 * 2, 2)
                        wdv = v3(wd_b.bf(), NFC)
                        wps = [AR.alloc("wgu%d" % i, 2 * 8 * 128 * 2) for i in range(3)]
                        sgs = [AR.alloc("sg%d" % i, 512 * 4) for i in range(3)]
                        sgi = 0
                        for c in range(NFC):
                            wp = wps[c % 3]
                            wpv = wp.bf().rearrange("p (e k m) -> p e k m", e=2, k=8)
                            S.dma("pool", wpv[:, 0, :, :], wgv[:, :, c * 128:(c + 1) * 128], writes=[wp.all()])
                            S.dma("pool", wpv[:, 1, :, :], wuv_[:, :, c * 128:(c + 1) * 128], writes=[wp.all()])
                            if c == 2:
                                wdsrc = w_down[l].rearrange("(c p) m -> p c m", p=128)
                                S.dma("pool", wdv[:, 0:11, :], wdsrc[:, 0:11, :], writes=[wd_b.b(0)])
                            if c == 5:
                                S.dma("pool", wdv[:, 11:22, :], wdsrc[:, 11:22, :], writes=[wd_b.b(1)])
                            for ti, tg in enumerate(blk):
                                n = tg.n; o_ = offs[ti]
                                bg = PS.get(); bu = PS.get()
                                for k in range(8):
                                    mm(bg.t[:, 0:n], wpv[:, 0, k, :], hbv[:, k, o_:o_ + n], k == 0, k == 7, [wp.all(), hTb.b(ti * 8 + k)], [bg.all()])
                                for k in range(8):
                                    mm(bu.t[:, 0:n], wpv[:, 1, k, :], hbv[:, k, o_:o_ + n], k == 0, k == 7, [wp.all(), hTb.b(ti * 8 + k)], [bu.all()])
                                sg = sgs[sgi % 3]; sgi += 1
                                S.op("act", lambda bg=bg, sg=sg, n=n: A.activation(out=sg.f32()[:, 0:n], in_=bg.t[:, 0:n], func=AF.Silu), reads=[bg.all()], writes=[sg.all()])
                                PS.put(bg)
                                S.op("dve", lambda bu=bu, sg=sg, c=c, o_=o_, n=n: V.tensor_tensor(out=actv[:, c, o_:o_ + n], in0=bu.t[:, 0:n], in1=sg.f32()[:, 0:n], op=ALU.mult),
                                     reads=[bu.all(), sg.all()], writes=[act.b(c)])
                                PS.put(bu)
                        for wp in wps:
                            AR.free(wp)
                        for b_ in sgs:
                            AR.free(b_)
                        AR.free(hTb)
                        if bi + 1 < len(blocks):
                            nxt = ffn_F0(blocks[bi + 1])
                        else:
                            wpg_b = load_w("wpg", 8 * 1024 * 2, [(lambda b: v3(b.bf(), 8), w_pgate[l].rearrange("(k p) m -> p k m", p=128))])
                            wpp_b = load_w("wpp", 2 * 1024 * 2, [(lambda b: v3(b.bf(), 2), w_pproj[l].rearrange("(k p) m -> p k m", p=128))])
                        for ti, tg in enumerate(blk):
                            n = tg.n; o_ = offs[ti]
                            f = AR.alloc("f", 8 * n * 4); fv = v3(f.f32(), 8)
                            bankf = PS.get()
                            sqs = [AR.alloc("f_sq%d" % i, n * 2) for i in range(2)]
                            prev_sq = None
                            for oc in range(8):
                                bk = PS.get()
                                for c in range(NFC):
                                    mm(bk.t[:, 0:n], wdv[:, c, oc * 128:(oc + 1) * 128], actv[:, c, o_:o_ + n], c == 0, c == NFC - 1, [wd_b.b(c // 11), act.b(c)], [bk.all()])
                                if prev_sq is not None:
                                    mm(bankf.t[:, 0:n], ones.bf(), prev_sq.bf(), oc == 1, False, [ones.all(), prev_sq.all()], [bankf.all()])
                                S.op("dve", lambda oc=oc, bk=bk: V.tensor_copy(out=fv[:, oc, :], in_=bk.t[:, 0:n]), reads=[bk.all()], writes=[f.all()])
                                PS.put(bk)
                                sq = sqs[oc % 2]
                                S.op("act", lambda oc=oc, sq=sq: A.activation(out=sq.bf(), in_=fv[:, oc, :], func=AF.Square, scale=1.0 / 32.0), reads=[f.all()], writes=[sq.all()])
                                prev_sq = sq
                            mm(bankf.t[:, 0:n], ones.bf(), prev_sq.bf(), False, True, [ones.all(), prev_sq.all()], [bankf.all()])
                            rf = AR.alloc("rf", n * 4)
                            S.op("act", lambda: A.activation(out=rf.f32(), in_=bankf.t[:, 0:n], func=AF.Ln, bias=epsb.f32()[:, 0:1], scale=1.0),
                                 reads=[bankf.all(), epsb.all()], writes=[rf.all()])
                            S.op("act", lambda: A.activation(out=rf.f32(), in_=rf.f32(), func=AF.Exp, scale=-0.5), reads=[rf.all()], writes=[rf.all()])
                            PS.put(bankf)
                            for q in sqs:
                                AR.free(q)
                            for oc in range(8):
                                S.op("dve", lambda oc=oc: V.scalar_tensor_tensor(out=fv[:, oc, :], in0=fv[:, oc, :], scalar=gfo.f32()[:, l * 8 + oc:l * 8 + oc + 1],
                                                                                 in1=rf.f32(), op0=ALU.mult, op1=ALU.mult),
                                     reads=[f.all(), gfo.all(), rf.all()], writes=[f.all()])
                                S.op("dve", lambda oc=oc, tg=tg: V.tensor_tensor(out=tg.x[:, oc, :], in0=tg.x[:, oc, :], in1=fv[:, oc, :], op=ALU.add),
                                     reads=[tg.xd, f.all()], writes=[tg.xd])
                            AR.free(rf); AR.free(f)
                        AR.free(act); AR.free(wd_b)
                        ckpt(6)

                    wpgv = v3(wpg_b.bf(), 8); wppv = v3(wpp_b.bf(), 2)
                    if l == 0:
                        pending_A[0] = load_A_weights(1)
                    elif s + 1 < NSEQ:
                        pending_A[0] = load_A_weights(0)
                    ptks = []
                    for tg in tgs:
                        ptk = AR.alloc("ptk", tg.ntile * 256 * 2); ptkv = v3(ptk.bf(), tg.ntile)
                        if tg.sample:
                            S.dma("pool", ptkv[0:tg.tp, 0, :], psm[l][:, :], writes=[ptk.all()])
                        else:
                            S.dma("pool", ptkv, pp[l, tg.s, tg.t0:tg.t0 + tg.n, :].rearrange("(a p) f -> p a f", p=128), writes=[ptk.all()])
                        ptks.append((ptk, ptkv))
                    for tgi, tg in enumerate(tgs):
                        n = tg.n; tp = tg.tp
                        xb = AR.alloc("xb", 8 * n * 2); xbv = v3(xb.bf(), 8)
                        S.op("dve", lambda tg=tg: V.tensor_copy(out=xbv[:, 0:4, :], in_=tg.x[:, 0:4, :]), reads=[tg.xd], writes=[xb.all()])
                        S.op("act", lambda tg=tg: A.activation(out=xbv[:, 4:8, :], in_=tg.x[:, 4:8, :], func=AF.Copy), reads=[tg.xd], writes=[xb.all()])
                        ptk, ptkv = ptks[tgi]
                        bT = PS.get(); bTb = bT.t[:, :].bitcast(BF16)
                        for i in range(tg.ntile):
                            for c in range(2):
                                S.op("pe", lambda i=i, c=c: PE.transpose(bTb[:, c * 512 + i * tp:c * 512 + (i + 1) * tp], ptkv[0:tp, i, c * 128:(c + 1) * 128],
                                                                         identb.bf()[0:tp, 0:tp]),
                                     reads=[ptk.all(), identb.all()], writes=[bT.all()])
                        pT_ = AR.alloc("pT_", 2 * n * 2); pTv = v3(pT_.bf(), 2)
                        for c in range(2):
                            S.op("act", lambda c=c: A.activation(out=pTv[:, c, :], in_=bTb[:, c * 512:c * 512 + n], func=AF.Copy), reads=[bT.all()], writes=[pT_.all()])
                        PS.put(bT); AR.free(ptk)
                        sgps = [AR.alloc("sgp%d" % i, n * 4) for i in range(2)]
                        for oc in range(8):
                            bg = PS.get(); bp = PS.get()
                            for k in range(8):
                                mm(bg.t[:, 0:n], wpgv[:, k, oc * 128:(oc + 1) * 128], xbv[:, k, :], k == 0, k == 7, [wpg_b.all(), xb.all()], [bg.all()])
                            for c in range(2):
                                mm(bp.t[:, 0:n], wppv[:, c, oc * 128:(oc + 1) * 128], pTv[:, c, :], c == 0, c == 1, [wpp_b.all(), pT_.all()], [bp.all()])
                            sg = sgps[oc % 2]
                            S.op("act", lambda bg=bg, sg=sg: A.activation(out=sg.f32(), in_=bg.t[:, 0:n], func=AF.Sigmoid), reads=[bg.all()], writes=[sg.all()])
                            PS.put(bg)
                            S.op("dve", lambda bp=bp, sg=sg: V.tensor_tensor(out=sg.f32(), in0=bp.t[:, 0:n], in1=sg.f32(), op=ALU.mult), reads=[bp.all(), sg.all()], writes=[sg.all()])
                            PS.put(bp)
                            S.op("dve", lambda oc=oc, tg=tg, sg=sg: V.tensor_tensor(out=tg.x[:, oc, :], in0=tg.x[:, oc, :], in1=sg.f32(), op=ALU.add),
                                 reads=[tg.xd, sg.all()], writes=[tg.xd])
                        AR.free(xb); AR.free(pT_)
                        for b_ in sgps:
                            AR.free(b_)
                    AR.free(wpg_b); AR.free(wpp_b)
                    ckpt(7)

                youts = [AR.alloc("yout%d" % i, 1024 * 4) for i in range(2)]

                def store_y(srcv, dep, tp, dst_ap, idx):
                    yb = youts[idx % 2]
                    for half in range(2):
                        bk = PS.get()
                        for cc in range(4):
                            c = half * 4 + cc
                            S.op("pe", lambda c=c, cc=cc, bk=bk: PE.transpose(bk.t[0:tp, cc * 128:(cc + 1) * 128], srcv[:, c, :], ident.f32()),
                                 reads=dep + [ident.all()], writes=[bk.all()])
                        if half == 0:
                            S.op("dve", lambda bk=bk, yb=yb: V.tensor_copy(out=yb.f32(0, tp)[:, 0:512], in_=bk.t[0:tp, :]), reads=[bk.all()], writes=[yb.all()])
                        else:
                            S.op("act", lambda bk=bk, yb=yb: A.activation(out=yb.f32(0, tp)[:, 512:1024], in_=bk.t[0:tp, :], func=AF.Copy), reads=[bk.all()], writes=[yb.all()])
                        PS.put(bk)
                    S.dma("sp", dst_ap, yb.f32(0, tp), reads=[yb.all()])

                for tt in range(NTT):
                    store_y(xTv[:, :, tt * 128:(tt + 1) * 128], [xT.b(tt // 4)], 128, y_p[s, tt * 128:(tt + 1) * 128, :], tt)
                if s == 0:
                    store_y(xsTv, [xsT.all()], NS, y_s[:, :], 0)
                for b in youts:
                    AR.free(b)


        except _Stop:
            pass
        S.finish()
        stats = dict(ninstr=dict(S.ninstr), nwait=dict(S.nwait), peak_bytes=AR.peak * 4)
    return nc, stats


def rope_tables(pos):
    half = 16
    inv = (10000.0 ** (-np.arange(half, dtype=np.float32) / np.float32(half))).astype(np.float32)
    ang = pos.astype(np.float32)[:, None] * inv[None, :]
    cos = np.cos(ang).astype(np.float32); sin = np.sin(ang).astype(np.float32)
    c32 = np.concatenate([cos.T, cos.T], axis=0)
    s32 = np.concatenate([-sin.T, sin.T], axis=0)
    c4 = np.ascontiguousarray(np.tile(c32, (4, 1))); s4 = np.ascontiguousarray(np.tile(s32, (4, 1)))
    return c4, s4, np.ascontiguousarray(cos), np.ascontiguousarray(sin)


_PROG_CACHE = {}


def run(inputs, NSEQ, T, PAST, ncores=8):
    key = (NSEQ, T, PAST)
    if key not in _PROG_CACHE:
        _PROG_CACHE[key] = build_program(NSEQ, T, PAST)
    nc, stats = _PROG_CACHE[key]
    f = lambda a: np.ascontiguousarray(np.asarray(a, dtype=np.float32))
    c4p, s4p, ctp, stp = rope_tables(np.arange(T))
    c4s, s4s, cts, sts = rope_tables(PAST + np.arange(32))
    wnames = ["g_mix_pre", "w_in", "w_conv", "g_q", "w_uq", "g_kv", "w_ukv", "g_conv_out", "g_attn_out", "w_o", "g_mix_post",
              "g_ffn_pre", "w_ffn_gate", "w_ffn_up", "w_ffn_down", "g_ffn_post", "w_ple_proj", "w_ple_gate"]
    shared = {k: f(inputs[k]) for k in wnames}
    shared.update(identd=np.eye(128, dtype=np.float32), c4p=c4p, s4p=s4p, c4s=c4s, s4s=s4s, ctp=ctp, stp=stp, cts=cts, sts=sts)
    xp = f(inputs["x_prompt"]); xs = f(inputs["x_sample"]); ck = f(inputs["cache_kv_latent"]); kr = f(inputs["cache_k_rope"])
    sc = f(inputs["state_conv"]); pp = f(inputs["p_prompt"]); ps = f(inputs["p_sample"])
    in_maps = []
    for c in range(ncores):
        m = dict(shared)
        m["xp"] = np.ascontiguousarray(xp[c * NSEQ:(c + 1) * NSEQ]); m["xs"] = np.ascontiguousarray(xs[c])
        m["ckvp"] = np.ascontiguousarray(ck[:, c]); m["krp"] = np.ascontiguousarray(kr[:, c]); m["cst"] = np.ascontiguousarray(sc[:, c])
        m["pp"] = np.ascontiguousarray(pp[:, c * NSEQ:(c + 1) * NSEQ]); m["psm"] = np.ascontiguousarray(ps[:, c])
        in_maps.append(m)
    res = run_bass_kernel_spmd(nc, in_maps, core_ids=list(range(ncores)))
    R = res.results
    y_p = np.concatenate([r["y_p"] for r in R], axis=0)
    y_s = np.stack([r["y_s"] for r in R], axis=0)
    lat_p = np.concatenate([r["lat_p"] for r in R], axis=1)
    kr_p = np.concatenate([r["kr_p"] for r in R], axis=1)
    conv_p = np.concatenate([r["conv_p"] for r in R], axis=1)
    lat_s = np.stack([r["lat_s"] for r in R], axis=1)
    kr_s = np.stack([r["kr_s"] for r in R], axis=1)
    conv_s = np.stack([r["conv_s"] for r in R], axis=1)
    return tuple(np.ascontiguousarray(a, dtype=np.float32) for a in (y_p, y_s, lat_p, kr_p, conv_p, lat_s, kr_s, conv_s))


def kernel(**inputs):
    return run(inputs, 4, 2048, 4096, 8)
```
